# Optimizing a Trainium2 kernel written in Bass

```python
import jax, jax.numpy as jnp
from jax import lax
import numpy as np

D_MODEL = 2048
BATCH = 16
SEQ = 2048
DEPTH = 1
DEC_BATCH = 32
DEC_SEQ = 32
PAST_LEN = 2048

CHUNK = 64
LEFT_CHUNKS = 8
LEFT_LEN = LEFT_CHUNKS * CHUNK
BAND = (LEFT_CHUNKS + 1) * CHUNK
N_HEADS = 16
HEAD_DIM = 64
ATTN_DIM = N_HEADS * HEAD_DIM
REL_CLIP = 128
N_REL = 2 * REL_CLIP + 1
CONV_DIM = D_MODEL // 2
CONV_WIDTH = 3
D_FF = 4 * D_MODEL
D_PLE = 256
IN_COLS = 3 * ATTN_DIM + 3 * CONV_DIM + 2 * D_MODEL
EPS = 1e-6

kernel_name = "hybrid_chunk_band_attn_shortconv_stream_step"


def rms_norm(x, g):
    xf = x.astype(jnp.float32)
    y = xf * lax.rsqrt(jnp.mean(xf * xf, axis=-1, keepdims=True) + EPS)
    return (y * g.astype(jnp.float32)).astype(x.dtype)


def split_proj(n, w_in):
    z = n @ w_in
    sizes = [ATTN_DIM, ATTN_DIM, ATTN_DIM, CONV_DIM, CONV_DIM, CONV_DIM, D_MODEL]
    idx = [int(i) for i in np.cumsum(sizes)]
    return jnp.split(z, idx, axis=-1)


def rel_bias(rel_table, rel):
    return rel_table[:, jnp.clip(rel, -REL_CLIP, REL_CLIP) + REL_CLIP]


def attend(q, k, v, bias, mask):
    s = jnp.einsum('bqhd,bkhd->bhqk', q, k).astype(jnp.float32) * (HEAD_DIM ** -0.5)
    s = s + bias.astype(jnp.float32)
    if mask is not None:
        s = jnp.where(mask, s, -1e30)
    p = jax.nn.softmax(s, axis=-1).astype(v.dtype)
    return jnp.einsum('bhqk,bkhd->bqhd', p, v)


def chunk_band_attention_prompt(q, k, v, rel_table):
    B, T = q.shape[0], q.shape[1]
    nc = T // CHUNK
    kp = jnp.pad(k, ((0, 0), (LEFT_LEN, 0), (0, 0), (0, 0)))
    vp = jnp.pad(v, ((0, 0), (LEFT_LEN, 0), (0, 0), (0, 0)))
    qi = jnp.arange(CHUNK)[:, None]
    kj = jnp.arange(BAND)[None, :]
    bias = rel_bias(rel_table, kj - LEFT_LEN - qi)
    qc = q.reshape(B, nc, CHUNK, N_HEADS, HEAD_DIM).transpose(1, 0, 2, 3, 4)

    def one_chunk(args):
        c, qb = args
        start = c * CHUNK
        kb = lax.dynamic_slice_in_dim(kp, start, BAND, axis=1)
        vb = lax.dynamic_slice_in_dim(vp, start, BAND, axis=1)
        mask = (start - LEFT_LEN + kj) >= 0
        return attend(qb, kb, vb, bias, mask)

    out = lax.map(one_chunk, (jnp.arange(nc), qc))
    return out.transpose(1, 0, 2, 3, 4).reshape(B, T, ATTN_DIM)


def chunk_band_attention_sample(q, k_new, v_new, k_cache, v_cache, rel_table):
    B, S = q.shape[0], q.shape[1]
    lc = k_cache.shape[1]
    k = jnp.concatenate([k_cache.astype(k_new.dtype), k_new], axis=1)
    v = jnp.concatenate([v_cache.astype(v_new.dtype), v_new], axis=1)
    qpos = jnp.arange(S)[:, None]
    kpos = jnp.arange(lc + S)[None, :] - lc
    bias = rel_bias(rel_table, kpos - qpos)
    return attend(q, k, v, bias, None).reshape(B, S, ATTN_DIM)


def causal_conv(u_ext, w, b):
    T = u_ext.shape[1] - (CONV_WIDTH - 1)
    y = b
    for j in range(CONV_WIDTH):
        y = y + w[j] * u_ext[:, j:j + T]
    return y


def layer_forward(x, pe, attn_fn, conv_left, w_in, w_attn_out, conv_w, conv_b, w_conv_out, w_o,
                  g_pre_mix, g_post_mix, g_pre_mlp, g_post_mlp, w_up, w_down, w_pe, w_pe_gate, g_pe):
    B, T = x.shape[0], x.shape[1]
    n = rms_norm(x, g_pre_mix)
    q, k, v, cb, cc, cx, ga, gb = split_proj(n, w_in)
    q = q.reshape(B, T, N_HEADS, HEAD_DIM)
    k = k.reshape(B, T, N_HEADS, HEAD_DIM)
    v = v.reshape(B, T, N_HEADS, HEAD_DIM)
    ya = attn_fn(q, k, v) @ w_attn_out
    u = cc * cx
    u_ext = jnp.concatenate([conv_left.astype(u.dtype), u], axis=1)
    yb = (cb * causal_conv(u_ext, conv_w, conv_b)) @ w_conv_out
    mix = (jax.nn.sigmoid(ga) * ya + jax.nn.sigmoid(gb) * yb) @ w_o
    h = x + rms_norm(mix, g_post_mix)
    f = jnp.square(jax.nn.relu(rms_norm(h, g_pre_mlp) @ w_up)) @ w_down
    h = h + rms_norm(f, g_post_mlp)
    gate = jax.nn.sigmoid(h @ w_pe_gate)
    h = h + rms_norm(gate * (pe @ w_pe), g_pe)
    conv_tail = u_ext[:, -(CONV_WIDTH - 1):]
    return h, k, v, conv_tail


def setup_inputs(seed: int = 0) -> dict:
    key = jax.random.key(seed)
    ks = jax.random.split(key, 32)
    f32 = jnp.float32
    lc = min(LEFT_LEN, PAST_LEN)

    def nrm(k, shape, scale):
        return jax.random.normal(k, shape, f32) * scale

    def gain(k):
        return 1.0 + 0.05 * jax.random.normal(k, (DEPTH, D_MODEL), f32)

    return {
        "x_prompt": nrm(ks[0], (BATCH, SEQ, D_MODEL), 1.0),
        "x_sample": nrm(ks[1], (DEC_BATCH, DEC_SEQ, D_MODEL), 1.0),
        "cache_k": nrm(ks[2], (DEPTH, DEC_BATCH, lc, N_HEADS, HEAD_DIM), 1.0),
        "cache_v": nrm(ks[3], (DEPTH, DEC_BATCH, lc, N_HEADS, HEAD_DIM), 1.0),
        "state_conv": nrm(ks[4], (DEPTH, DEC_BATCH, CONV_WIDTH - 1, CONV_DIM), 1.0),
        "p_prompt": nrm(ks[5], (DEPTH, BATCH, SEQ, D_PLE), 1.0),
        "p_sample": nrm(ks[6], (DEPTH, DEC_BATCH, DEC_SEQ, D_PLE), 1.0),
        "w_in": nrm(ks[7], (DEPTH, D_MODEL, IN_COLS), D_MODEL ** -0.5),
        "rel_table": nrm(ks[8], (DEPTH, N_HEADS, N_REL), 0.2),
        "w_attn_out": nrm(ks[9], (DEPTH, ATTN_DIM, D_MODEL), ATTN_DIM ** -0.5),
        "conv_w": nrm(ks[10], (DEPTH, CONV_WIDTH, CONV_DIM), CONV_WIDTH ** -0.5),
        "conv_b": nrm(ks[11], (DEPTH, CONV_DIM), 0.01),
        "w_conv_out": nrm(ks[12], (DEPTH, CONV_DIM, D_MODEL), CONV_DIM ** -0.5),
        "w_o": nrm(ks[13], (DEPTH, D_MODEL, D_MODEL), D_MODEL ** -0.5),
        "g_pre_mix": gain(ks[14]),
        "g_post_mix": gain(ks[15]),
        "g_pre_mlp": gain(ks[16]),
        "g_post_mlp": gain(ks[17]),
        "w_up": nrm(ks[18], (DEPTH, D_MODEL, D_FF), D_MODEL ** -0.5),
        "w_down": nrm(ks[19], (DEPTH, D_FF, D_MODEL), D_FF ** -0.5),
        "w_pe": nrm(ks[20], (DEPTH, D_PLE, D_MODEL), D_PLE ** -0.5),
        "w_pe_gate": nrm(ks[21], (DEPTH, D_MODEL, D_MODEL), D_MODEL ** -0.5),
        "g_pe": gain(ks[22]),
    }


def reference(x_prompt, x_sample, cache_k, cache_v, state_conv, p_prompt, p_sample,
              w_in, rel_table, w_attn_out, conv_w, conv_b, w_conv_out, w_o,
              g_pre_mix, g_post_mix, g_pre_mlp, g_post_mlp, w_up, w_down, w_pe, w_pe_gate, g_pe):
    hp, hs = x_prompt, x_sample
    kp_l, vp_l, cp_l, ks_l, vs_l, cs_l = [], [], [], [], [], []
    lp = min(LEFT_LEN, x_prompt.shape[1])
    for i in range(DEPTH):
        w = (w_in[i], w_attn_out[i], conv_w[i], conv_b[i], w_conv_out[i], w_o[i],
             g_pre_mix[i], g_post_mix[i], g_pre_mlp[i], g_post_mlp[i],
             w_up[i], w_down[i], w_pe[i], w_pe_gate[i], g_pe[i])
        rt = rel_table[i]

        def attn_prompt(q, k, v, rt=rt):
            return chunk_band_attention_prompt(q, k, v, rt)

        def attn_sample(q, k, v, rt=rt, kc=cache_k[i], vc=cache_v[i]):
            return chunk_band_attention_sample(q, k, v, kc, vc, rt)

        zero_left = jnp.zeros((hp.shape[0], CONV_WIDTH - 1, CONV_DIM), hp.dtype)
        hp, kpr, vpr, cpr = layer_forward(hp, p_prompt[i], attn_prompt, zero_left, *w)
        hs, ksm, vsm, csm = layer_forward(hs, p_sample[i], attn_sample, state_conv[i], *w)
        kp_l.append(kpr[:, -lp:]); vp_l.append(vpr[:, -lp:]); cp_l.append(cpr)
        ks_l.append(ksm); vs_l.append(vsm); cs_l.append(csm)
    k_prompt = jnp.stack(kp_l); v_prompt = jnp.stack(vp_l); conv_prompt = jnp.stack(cp_l)
    k_sample = jnp.stack(ks_l); v_sample = jnp.stack(vs_l); conv_sample = jnp.stack(cs_l)
    return (hp, hs, k_prompt, v_prompt, conv_prompt, k_sample, v_sample, conv_sample)
```

```python
import os as _os
import numpy as np
import concourse.bass as bass
import concourse.mybir as mybir
from concourse.bass_utils import run_bass_kernel_spmd
from contextlib import ExitStack

F32 = mybir.dt.float32
BF16 = mybir.dt.bfloat16
U8 = mybir.dt.uint8
AF = mybir.ActivationFunctionType
ALU = mybir.AluOpType

ENGS = ("pe", "act", "dve", "pool", "sp")
NDMA_SLOTS = 8


class Buf:
    __slots__ = ("name", "last_w", "rc", "rd", "excl")

    def __init__(self, name, excl=False):
        self.name = name
        self.excl = excl
        self.last_w = None
        self.rc = {}
        self.rd = []


class Op:
    __slots__ = ("eng", "fn", "cdeps", "ddeps", "sig", "signo", "dma", "dslot", "dcnt", "pos")

    def __init__(self, eng, fn, dma, pos):
        self.eng = eng
        self.fn = fn
        self.cdeps = {}
        self.ddeps = {}
        self.sig = False
        self.signo = 0
        self.dma = dma
        self.dslot = 0
        self.dcnt = 0
        self.pos = pos


EPOCH = 2048
DMA_GEN = 60


class Sched:
    def __init__(self):
        self.ops = {e: [] for e in ENGS}

    def add(self, eng, fn, reads=(), writes=(), dma=False):
        op = Op(eng, fn, dma, len(self.ops[eng]))
        if any(b.excl for b in reads):
            writes = list(writes) + [b for b in reads if b.excl and b not in writes]
            reads = [b for b in reads if not b.excl]

        def dep(d):
            if d is op:
                return
            if d.dma:
                op.ddeps[d] = 1
            else:
                cur = op.cdeps.get(d.eng)
                if cur is None or d.pos > cur.pos:
                    op.cdeps[d.eng] = d

        for b in reads:
            if b.last_w is not None:
                dep(b.last_w)
        for b in writes:
            if b.last_w is not None:
                dep(b.last_w)
            for r in b.rc.values():
                dep(r)
            for r in b.rd:
                dep(r)
        for b in writes:
            b.last_w = op
            b.rc = {}
            b.rd = []
        for b in reads:
            if b.last_w is not op:
                if dma:
                    b.rd.append(op)
                else:
                    b.rc[eng] = op
        self.ops[eng].append(op)
        return op

    @staticmethod
    def _needs_wait(op, dep):
        if dep.dma or op.dma:
            return True
        if dep.eng != op.eng:
            return True
        return op.eng != "pe"

    def finalize(self):
        for e in ENGS:
            for op in self.ops[e]:
                for d in list(op.cdeps.values()) + list(op.ddeps):
                    if self._needs_wait(op, d):
                        d.sig = True
        nep = {}
        for e in ENGS:
            n = 0
            nd = 0
            for op in self.ops[e]:
                if op.dma:
                    r = nd // NDMA_SLOTS
                    op.dslot = (r // DMA_GEN) * NDMA_SLOTS + nd % NDMA_SLOTS
                    op.dcnt = r % DMA_GEN + 1
                    nd += 1
                elif op.sig:
                    n += 1
                    op.signo = n
            nep[e] = (n + EPOCH - 1) // EPOCH
            nep["d_" + e] = ((nd + NDMA_SLOTS - 1) // NDMA_SLOTS + DMA_GEN - 1) // DMA_GEN * NDMA_SLOTS
        return nep

    def emit(self, nc, block, esem, dsem):
        engobj = {"pe": nc.tensor, "act": nc.scalar, "dve": nc.vector,
                  "pool": nc.gpsimd, "sp": nc.sync}
        deco = {"pe": block.tensor, "act": block.scalar, "dve": block.vector,
                "pool": block.gpsimd, "sp": block.sync}

        def run_engine(e):
            eng = engobj[e]
            waited = {}

            def wait(sem, val, key):
                if waited.get(key, 0) >= val:
                    return
                waited[key] = val
                eng.wait_ge(sem, val)

            def wait_c(d):
                ep, v = divmod(d.signo - 1, EPOCH)
                for k in list(waited):
                    pass
                wait(esem[d.eng][ep], v + 1, ("e", d.eng, ep))

            last_d = {}
            prev_in_lane = {}
            for op in self.ops[e]:
                for d in op.cdeps.values():
                    if self._needs_wait(op, d):
                        wait_c(d)
                for d in op.ddeps:
                    wait(dsem[d.eng][d.dslot], 16 * d.dcnt, ("d", d.eng, d.dslot))
                if op.dma:
                    prev = prev_in_lane.get(op.dslot % NDMA_SLOTS)
                    if prev is not None:
                        wait(dsem[e][prev.dslot], 16 * prev.dcnt, ("d", e, prev.dslot))
                    prev_in_lane[op.dslot % NDMA_SLOTS] = op
                    ins = op.fn(eng)
                    ins.then_inc(dsem[e][op.dslot], 16)
                    last_d[op.dslot] = op.dcnt
                else:
                    ins = op.fn(eng)
                    if op.sig:
                        ep = (op.signo - 1) // EPOCH
                        ins.then_inc(esem[e][ep], 1)
            for slot, cnt in last_d.items():
                wait(dsem[e][slot], 16 * cnt, ("d", e, slot))

        for e in ENGS:
            if not self.ops[e]:
                continue

            def _f(_eng, e=e):
                run_engine(e)
            deco[e](_f)


D = 2048
DC = 16
ATT = 1024
NH = 16
HD = 64
CONV = 1024
DFF = 8192
DPLE = 256
INC = 10240
EPS = 1e-6
NEG = -30000.0
NSLOT = 4
SLOT = 4096
RBYTES = 49152
NBLK = 153
TT_P = 512


def _prod(s):
    r = 1
    for v in s:
        r *= v
    return r


class Chunked:
    def __init__(self, ap, bufs):
        self.ap = ap
        self.bufs = bufs

    def __getitem__(self, c):
        return self.ap[:, c]

    def b(self, c):
        return self.bufs[c]

    def allb(self):
        out = []
        for x in self.bufs:
            for y in x:
                if y not in out:
                    out.append(y)
        return out


class _Stop(Exception):
    pass


def build_program(n_prompt_tiles=8, do_sample=True, stop=None):
    nc = bass.Bass("TRN2", target_bir_lowering=False)
    dtn = nc.dram_tensor

    def din(name, shape):
        return dtn(name, list(shape), F32, kind="ExternalInput").ap()

    def dout(name, shape):
        return dtn(name, list(shape), F32, kind="ExternalOutput").ap()

    xp = din("xp", [2, 2048, D])
    xs = din("xs", [128, D])
    ck = din("ck", [4, 512, ATT])
    cv = din("cv", [4, 512, ATT])
    sc = din("sc", [8, CONV])
    pp = din("pp", [2, 2048, DPLE])
    ps_ = din("ps", [128, DPLE])
    w_in = din("w_in", [D, INC])
    rel = din("rel", [NH, 257])
    w_ao = din("w_ao", [ATT, D])
    cw = din("cw", [3, CONV])
    cb_ = din("cb", [1, CONV])
    w_co = din("w_co", [CONV, D])
    w_o = din("w_o", [D, D])
    gains = din("gains", [5, D])
    w_up = din("w_up", [D, DFF])
    w_dn = din("w_dn", [DFF, D])
    w_pe = din("w_pe", [DPLE, D])
    w_pg = din("w_pg", [D, D])

    yp = dout("yp", [2, 2048, D])
    ys = dout("ys", [128, D])
    kp = dout("kp", [2, 512, ATT])
    vp = dout("vp", [2, 512, ATT])
    cp = dout("cp", [2, 2, CONV])
    ks = dout("ks", [128, ATT])
    vs = dout("vs", [128, ATT])
    cs = dout("cs", [8, CONV])

    wscr = dtn("wscr", [NBLK, 128, SLOT], BF16, kind="Internal").ap()
    text = dtn("text", [NH, 400], F32, kind="Internal").ap()

    S = Sched()
    st = ExitStack()
    with st:
        def sb(name, shape, dt):
            return st.enter_context(nc.sbuf_tensor(name, list(shape), dt))

        XTt = sb("XT", [128, DC, 512], F32)
        NTt = sb("NT", [128, DC, 512], BF16)
        KTt = [sb(f"KT{i}", [128, 8, 512], BF16) for i in range(2)]
        VVt = [sb(f"VV{i}", [128, 4, NH, 65], BF16) for i in range(2)]
        Rt = sb("R", [128, RBYTES], U8)
        WSt = [sb(f"WS{i}", [128, SLOT], BF16) for i in range(NSLOT)]
        KVSt = [sb(f"KVS{i}", [128, 512], F32) for i in range(2)]
        SQt = [sb(f"SQ{i}", [128, 512], BF16) for i in range(3)]
        RSt = sb("RS", [128, 512], F32)
        RSTDt = sb("RSTD", [128, 512], F32)
        NTMP = 5
        TMPt = [sb(f"TMP{i}", [128, 512], F32) for i in range(NTMP)]
        NE = 6
        Et = [sb(f"E{i}", [128, 512], BF16) for i in range(NE)]
        BTt = sb("BT", [128, NH, 2, 128], BF16)
        M0t = sb("M0", [128, 128], BF16)
        IDbt = sb("IDb", [128, 128], BF16)
        IDft = sb("IDf", [128, 128], F32)
        Jft = sb("Jf", [128, 128], F32)
        ONESt = sb("ONES", [128, 128], BF16)
        PTt = sb("PT", [128, 2, 512], BF16)
        PSTt = [sb(f"PST{i}", [128, DPLE], F32) for i in range(2)]
        SMt = sb("SM", [128, 128], F32)
        SMTt = sb("SMT", [128, 128], F32)
        CHt = sb("CH", [128, NH], F32)
        CARRYt = sb("CARRY", [128, 8, 8], F32)
        COt = sb("CO", [128, 8, 8], F32)
        COSt = sb("COS", [8, CONV], F32)
        RDt = sb("RD", [128, 4], F32)
        BPt = sb("BP", [128, 2, 128], F32)

        PB = [st.enter_context(nc.psum_tensor(f"PB{i}", [128, 512], F32)) for i in range(8)]
        bPB = [Buf(f"pb{i}", excl=True) for i in range(8)]


        def mk(name, n):
            return [[Buf(f"{name}{i}")] for i in range(n)]

        XT = Chunked(XTt[:], mk("xt", DC))
        NT = Chunked(NTt[:], mk("nt", DC))
        KT = [Chunked(KTt[i][:], mk(f"kt{i}_", 8)) for i in range(2)]
        bVV = [[Buf(f"vv{i}_{j}") for j in range(4)] for i in range(2)]
        rpages = [Buf(f"rp{i}") for i in range(RBYTES // 1024)]
        bWS = [Buf(f"ws{i}") for i in range(NSLOT)]
        bKVS = [Buf(f"kvs{i}") for i in range(2)]
        bSQ = [Buf(f"sq{i}") for i in range(3)]
        bRS, bRSTD = Buf("rs"), Buf("rstd")
        bTMP = [Buf(f"tmp{i}") for i in range(NTMP)]
        bE = [Buf(f"e{i}") for i in range(NE)]
        bBT, bM0, bIDb, bIDf, bJf, bONES = (Buf(n) for n in "bt m0 idb idf jf ones".split())
        bPT = Buf("pt")
        bPST = [Buf(f"pst{i}") for i in range(2)]
        bSM, bSMT, bCH, bCARRY, bCO, bCOS, bRD, bBP = (Buf(n) for n in "sm smt ch carry co cos rd bp".split())
        bTEXT = Buf("text")
        bWSCR = [Buf(f"wscr{i}") for i in range(NBLK)]
        bOUT = Buf("out")

        def rview(off, dt, shape):
            isz = 4 if dt == F32 else 2
            nbytes = _prod(shape) * isz
            assert off + nbytes <= RBYTES and off % 4 == 0, (off, nbytes)
            ap = Rt[:, off:off + nbytes].bitcast(dt)
            if len(shape) == 2:
                ap = ap.rearrange("p (a b) -> p a b", a=shape[0])
            elif len(shape) == 3:
                ap = ap.rearrange("p (a b c) -> p a b c", a=shape[0], b=shape[1])
            return ap

        def rpg(off, nbytes):
            return rpages[off // 1024:(off + nbytes + 1023) // 1024]

        def rchunked(off, dt, nchunk, celems):
            isz = 4 if dt == F32 else 2
            ap = rview(off, dt, [nchunk, celems])
            bufs = [rpg(off + c * celems * isz, celems * isz) for c in range(nchunk)]
            return Chunked(ap, bufs)

        tmp_i = [0]

        def tmp():
            i = tmp_i[0] % NTMP
            tmp_i[0] += 1
            return TMPt[i], bTMP[i]

        gb_i = [0]

        def gbank():
            i = gb_i[0] % 5
            gb_i[0] += 1
            return PB[i], bPB[i]

        sq_i = [0]
        e_i = [0]
        pv_i = [0]
        kvs_i = [0]

        def mm(out, lhsT, rhs, start, stop, reads, writes):
            S.add("pe", lambda e: e.matmul(out, lhsT=lhsT, rhs=rhs, start=start, stop=stop),
                  reads=reads, writes=writes)

        def tr(out, in_, ident, reads, writes):
            S.add("pe", lambda e: e.transpose(out, in_, ident), reads=reads, writes=writes)

        def act(out, in_, func, reads, writes, scale=1.0, bias=0.0):
            S.add("act", lambda e: e.activation(out, in_, func, bias=bias, scale=scale),
                  reads=reads, writes=writes)

        def dma(eng, out, in_, reads, writes, slow=False):
            if slow:
                S.add(eng, lambda e: e.dma_start(out=out, in_=in_, allow_slow_non_contiguous=True),
                      reads=reads, writes=writes, dma=True)
            else:
                S.add(eng, lambda e: e.dma_start(out=out, in_=in_), reads=reads, writes=writes, dma=True)

        S.add("pool", lambda e: e.memset(IDft[:], 0.0), writes=[bIDf])
        S.add("pool", lambda e: e.affine_select(IDft[:], IDft[:], [[-1, 128]], ALU.not_equal, 1.0,
                                                base=0, channel_multiplier=1), reads=[bIDf], writes=[bIDf])
        S.add("pool", lambda e: e.memset(Jft[:], 0.0), writes=[bJf])
        S.add("pool", lambda e: e.affine_select(Jft[:], Jft[:], [[1, 128]], ALU.not_equal, 1.0,
                                                base=-127, channel_multiplier=1), reads=[bJf], writes=[bJf])
        S.add("dve", lambda e: e.tensor_copy(IDbt[:], IDft[:]), reads=[bIDf], writes=[bIDb])
        S.add("pool", lambda e: e.memset(ONESt[:], 1.0 / D), writes=[bONES])
        S.add("pool", lambda e: e.memset(M0t[:], 0.0), writes=[bM0])
        S.add("pool", lambda e: e.memset(M0t[0:64, 64:128], NEG), reads=[bM0], writes=[bM0])
        for i in range(2):
            S.add("pool", lambda e, i=i: e.memset(VVt[i][:, :, :, 64:65], 2.0), writes=bVV[i])
        S.add("pool", lambda e: e.memset(CARRYt[:], 0.0), writes=[bCARRY])

        S.add("pool", lambda e: e.memset(SMt[:], 0.0), writes=[bSM])
        for n in range(5):
            dma("sp", SMt[n * 16:(n + 1) * 16, :], gains[n].rearrange("(c p) -> c p", p=128), [], [bSM])
        for j in range(3):
            dma("pool", SMt[80 + j * 8:88 + j * 8, :], cw[j].rearrange("(c p) -> c p", p=128), [], [bSM])
        dma("pool", SMt[104:112, :], cb_[0].rearrange("(c p) -> c p", p=128), [], [bSM])
        tr(PB[0][:, 0:128], SMt[:], IDft[:], [bSM, bIDf], [bPB[0]])
        S.add("dve", lambda e: e.tensor_copy(SMTt[:], PB[0][:, 0:128]), reads=[bPB[0]], writes=[bSMT])
        S.add("dve", lambda e: e.tensor_scalar(SMTt[:, 80:112], SMTt[:, 80:112], 0.5, 0.0, ALU.mult, ALU.add),
              reads=[bSMT], writes=[bSMT])

        def gain(n, c):
            return SMTt[:, n * 16 + c:n * 16 + c + 1]

        def convw(j, c):
            return SMTt[:, 80 + j * 8 + c:80 + j * 8 + c + 1]

        def convb(c):
            return SMTt[:, 104 + c:105 + c]

        S.add("pool", lambda e: e.memset(TMPt[0][:], 0.0), writes=[bTMP[0]])
        dma("sp", text[:, 0:128], TMPt[0][0:NH, 0:128], [bTMP[0]], [bTEXT])
        dma("sp", text[:, 128:385], rel, [], [bTEXT])
        dma("sp", CHt[:], bass.AP(rel.tensor, 0, [[0, 128], [257, NH]]), [], [bCH], slow=True)
        for h in range(NH):
            dma("sp", BPt[:], bass.AP(text.tensor, h * 400 + 1, [[1, 128], [128, 2], [1, 128]]), [bTEXT], [bBP])
            for blk in range(2):
                tr(PB[1][:, blk * 128:(blk + 1) * 128], BPt[:, blk, :], Jft[:], [bBP, bJf], [bPB[1]])
            tb, btb = TMPt[1 + (h % 2)], bTMP[1 + (h % 2)]
            S.add("dve", lambda e, h=h, tb=tb: e.tensor_scalar(
                tb[:, 0:256], PB[1][:, 0:256], CHt[:, h:h + 1], 1.0, ALU.subtract, ALU.mult),
                reads=[bPB[1], bCH], writes=[btb])
            S.add("pool", lambda e, tb=tb: e.affine_select(tb[:, 0:128], tb[:, 0:128], [[-1, 128]], ALU.is_ge, 0.0,
                                                           base=0, channel_multiplier=1), reads=[btb], writes=[btb])
            S.add("pool", lambda e, tb=tb: e.memset(tb[64:128, 128:192], NEG), reads=[btb], writes=[btb])
            S.add("dve", lambda e, h=h, tb=tb: e.tensor_copy(
                BTt[:, h, :, :], tb[:, 0:256].rearrange("p (b k) -> p b k", b=2)), reads=[btb], writes=[bBT])

        def chk(name):
            if stop == name:
                raise _Stop()

        wstate = {"g": 0, "issued": 0, "seq": None, "tiles": 0}

        def wsrc(desc):
            kind = desc[0]
            if kind == "std":
                _, W, r0, kc, c0, ncols = desc
                src = W[r0:r0 + kc * 128, c0:c0 + ncols].rearrange("(k p) f -> p k f", p=128)
                return src, [kc, ncols]
            if kind == "conv":
                j = desc[1]
                src = bass.AP(w_in.tensor, 4096 + 128 * j, [[INC, 128], [128 * INC, 16], [1024, 2], [1, 128]])
                return src, [16, 2, 128]
            raise ValueError(kind)

        def slot_view(si, shp):
            n = _prod(shp)
            ap = WSt[si][:, 0:n]
            if len(shp) == 2:
                return ap.rearrange("p (k f) -> p k f", k=shp[0])
            return ap.rearrange("p (k s f) -> p k s f", k=shp[0], s=shp[1])

        def issue_load(gidx):
            seq = wstate["seq"]
            ti, bi = divmod(gidx, len(seq))
            desc = seq[bi]
            si = gidx % NSLOT
            src, shp = wsrc(desc)
            n = _prod(shp)
            from_f32 = ti == 0 or (ti == 1 and bi % 2 == 0)
            writeback = (ti == 0 and bi % 2 == 1) or (ti == 1 and bi % 2 == 0) or wstate["tiles"] == 1
            if from_f32 and desc[0] == "conv":
                for seg in range(2):
                    srcs = bass.AP(w_in.tensor, 4096 + 128 * desc[1] + 1024 * seg,
                                   [[INC, 128], [128 * INC, 16], [1, 128]])
                    dma("pool", slot_view(si, shp)[:, :, seg, :], srcs, [], [bWS[si]])
            elif from_f32:
                dma("pool", slot_view(si, shp), src, [], [bWS[si]])
            else:
                dma("pool", WSt[si][:, 0:n], wscr[bi, :, 0:n], [bWSCR[bi]], [bWS[si]])
            if writeback and wstate["tiles"] > 1:
                dma("sp", wscr[bi, :, 0:n], WSt[si][:, 0:n], [bWS[si]], [bWSCR[bi]])

        def next_block(desc):
            seq = wstate["seq"]
            g = wstate["g"]
            assert seq[g % len(seq)] == desc, (g, seq[g % len(seq)], desc)
            total = len(seq) * wstate["tiles"]
            while wstate["issued"] < min(g + NSLOT, total):
                issue_load(wstate["issued"])
                wstate["issued"] += 1
            wstate["g"] = g + 1
            si = g % NSLOT
            _, shp = wsrc(desc)
            return slot_view(si, shp), bWS[si]

        def tile_wseq():
            seq = []
            for c0 in range(0, 1024, 256):
                seq.append(("std", w_in, 0, 16, c0, 256))
            for c0 in range(0, 1024, 256):
                seq.append(("std", w_in, 0, 16, 1024 + c0, 256))
            for c0 in range(0, 1024, 256):
                seq.append(("std", w_in, 0, 16, 2048 + c0, 256))
            for j in range(0, 8, 2):
                seq.append(("conv", j))
                seq.append(("conv", j + 1))
                seq.append(("std", w_in, 0, 16, 3072 + j * 128, 256))
            for j in range(0, 16, 2):
                seq.append(("std", w_in, 0, 16, 6144 + j * 128, 256))
                seq.append(("std", w_ao, 0, 8, j * 128, 256))
                seq.append(("std", w_in, 0, 16, 8192 + j * 128, 256))
                seq.append(("std", w_co, 0, 8, j * 128, 256))
            for j in range(0, 16, 2):
                seq.append(("std", w_o, 0, 16, j * 128, 256))
            for q in range(4):
                for j in range(0, 16, 2):
                    seq.append(("std", w_up, 0, 16, q * 2048 + j * 128, 256))
                for j in range(0, 16, 2):
                    seq.append(("std", w_dn, q * 2048, 16, j * 128, 256))
            seq.append(("std", w_pe, 0, 2, 0, 2048))
            for j in range(0, 16, 2):
                seq.append(("std", w_pg, 0, 16, j * 128, 256))
            return seq

        wstate["seq"] = tile_wseq()
        assert len(wstate["seq"]) <= NBLK, len(wstate["seq"])
        wstate["tiles"] = n_prompt_tiles + (1 if do_sample else 0)

        ss_pend = []
        SS_LAG = 2

        def ss_add(T, c, ap, bufs):
            i = sq_i[0] % 3
            sq_i[0] += 1
            act(SQt[i][:, 0:T], ap, AF.Square, bufs, [bSQ[i]])
            ss_pend.append((T, c, i))
            while len(ss_pend) > SS_LAG:
                ss_flush_one()

        def ss_flush_one():
            T, c, i = ss_pend.pop(0)
            mm(PB[5][:, 0:T], ONESt[:], SQt[i][:, 0:T], c == 0, c == DC - 1, [bONES, bSQ[i]], [bPB[5]])

        def ss_finish(T, eps):
            while ss_pend:
                ss_flush_one()
            act(RSt[:, 0:T], PB[5][:, 0:T], AF.Sqrt, [bPB[5]], [bRS], bias=eps)
            S.add("dve", lambda e: e.reciprocal(RSTDt[:, 0:T], RSt[:, 0:T]), reads=[bRS], writes=[bRSTD])

        def prescale_x(T, n, c):
            S.add("dve", lambda e, c=c: e.tensor_scalar(
                NT[c][:, 0:T], XT[c][:, 0:T], gain(n, c), 1.0, ALU.mult, ALU.mult),
                reads=XT.b(c) + [bSMT], writes=NT.b(c))

        def finish_norm(T):
            for c in range(DC):
                S.add("dve", lambda e, c=c: e.tensor_tensor(
                    NT[c][:, 0:T], NT[c][:, 0:T], RSTDt[:, 0:T], ALU.mult),
                    reads=NT.b(c) + [bRSTD], writes=NT.b(c))

        def residual_add(T, srcg, after=None):
            for c in range(DC):
                t, bt = tmp()
                S.add("dve", lambda e, c=c, t=t: e.tensor_tensor(
                    t[:, 0:T], srcg[c][:, 0:T], RSTDt[:, 0:T], ALU.mult),
                    reads=srcg.b(c) + [bRSTD], writes=[bt])
                S.add("dve" if c % 2 == 0 else "pool", lambda e, c=c, t=t: e.tensor_tensor(
                    XT[c][:, 0:T], XT[c][:, 0:T], t[:, 0:T], ALU.add),
                    reads=XT.b(c) + [bt], writes=XT.b(c))
                if after is not None:
                    after(c)

        def proj_fm(T, wv, bw, kc, jj, rhs_of, rhs_bufs_of):
            bank, bb = gbank()
            for k in range(kc):
                mm(bank[:, 0:T], wv[:, k, jj * 128:(jj + 1) * 128], rhs_of(k), k == 0, k == kc - 1,
                   [bw] + rhs_bufs_of(k), [bb])
            return bank, bb

        xpre = {"on": False}

        def do_tile(tix, sample, nxt=None):
            T = 128 if sample else 512
            nb, L = (4, 32) if sample else (1, 512)
            if sample:
                seq, t = None, None
                cur, prv = 1, 0
                groups = [(32 * b, 32) for b in range(4)]
                xsrc, psrc = xs, ps_
                ydst = ys
                lay = dict(QT=0, U=2048, YBIN=6400, AOT=8448, ATK=10496, CKS=16384, CVS=24576,
                           MERGE=32768, MIX=36864)
            else:
                seq, t = divmod(tix, 4)
                cur, prv = t % 2, (t + 1) % 2
                groups = [(128 * s, 128) for s in range(4)]
                xsrc, psrc = xp[seq, t * 512:(t + 1) * 512, :], pp[seq, t * 512:(t + 1) * 512, :]
                ydst = yp[seq, t * 512:(t + 1) * 512, :]
                lay = dict(QT=0, U=8192, YBIN=24704, AOT=32896, ATK=41088, MERGE=0, MIX=16384)
            out_tile = sample or (t == 3 and not _os.environ.get('K_NOOUT'))
            osel = 'kvc' if sample else _os.environ.get('K_OUTSEL', 'kvc')
            NS = T // 128
            UW = nb * (L + 2)

            QT = rchunked(lay["QT"], BF16, 8, T)
            Uc = rchunked(lay["U"], F32, 8, UW)
            YBIN = rchunked(lay["YBIN"], BF16, 8, T)
            AOT = rchunked(lay["AOT"], BF16, 8, T)
            ATK = [(rview(lay["ATK"] + i * 2048, BF16, [NH, HD]), rpg(lay["ATK"] + i * 2048, 2048)) for i in range(2)]
            MERGE = rchunked(lay["MERGE"], BF16, DC, T)
            MIX = rchunked(lay["MIX"], F32, DC, T)
            UPQ = rchunked(0, BF16, 16, T)
            Fc = rchunked(16384, F32, DC, T)
            Gc = rchunked(0, F32, DC, T)
            IOST = [(rview(32768 + i * 8192, F32, [D]), rpg(32768 + i * 8192, 8192)) for i in range(2)]

            pre0 = xpre["on"]
            xpre["on"] = False
            for s in range(NS):
                io, bio = IOST[s % 2]
                if not (pre0 and s == 0):
                    dma("sp", io, xsrc[s * 128:(s + 1) * 128, :], [], bio)
                pst, bpst = PSTt[s % 2], bPST[s % 2]
                dma("sp", pst[:], psrc[s * 128:(s + 1) * 128, :], [], [bpst])
                for cg in range(4):
                    bank, bb = gbank()
                    for ci in range(4):
                        c = cg * 4 + ci
                        if pre0 and s == 0:
                            tr(bank[:, ci * 128:(ci + 1) * 128], TMPt[cg][:, ci * 128:(ci + 1) * 128], IDft[:],
                               [bTMP[cg], bIDf], [bb])
                        else:
                            tr(bank[:, ci * 128:(ci + 1) * 128], io[:, c * 128:(c + 1) * 128], IDft[:],
                               bio + [bIDf], [bb])
                    wb = []
                    for ci in range(4):
                        wb += XT.b(cg * 4 + ci)
                    S.add("dve", lambda e, bank=bank, cg=cg, s=s: e.tensor_copy(
                        XTt[:, cg * 4:(cg + 1) * 4, s * 128:(s + 1) * 128],
                        bank[:].rearrange("p (c t) -> p c t", c=4)), reads=[bb], writes=wb)
                bank, bb = gbank()
                for c in range(2):
                    tr(bank[:, c * 128:(c + 1) * 128], pst[:, c * 128:(c + 1) * 128], IDft[:], [bpst, bIDf], [bb])
                S.add("dve", lambda e, bank=bank, s=s: e.tensor_copy(
                    PTt[:, :, s * 128:(s + 1) * 128], bank[:, 0:256].rearrange("p (c t) -> p c t", c=2)),
                    reads=[bb], writes=[bPT])

            chk("A")
            for c in range(DC):
                ss_add(T, c, XT[c][:, 0:T], XT.b(c))
                prescale_x(T, 0, c)
            ss_finish(T, EPS)
            finish_norm(T)

            def nt_rhs(k):
                return NT[k][:, 0:T]

            def nt_b(k):
                return NT.b(k)

            chk("norm1")
            for blk in range(4):
                wv, bw = next_block(("std", w_in, 0, 16, blk * 256, 256))
                for jj in range(2):
                    j = blk * 2 + jj
                    bank, bb = proj_fm(T, wv, bw, 16, jj, nt_rhs, nt_b)
                    act(QT[j][:, 0:T], bank[:, 0:T], AF.Copy, [bb], QT.b(j), scale=0.125)
            for blk in range(4):
                wv, bw = next_block(("std", w_in, 0, 16, 1024 + blk * 256, 256))
                for jj in range(2):
                    j = blk * 2 + jj
                    bank, bb = proj_fm(T, wv, bw, 16, jj, nt_rhs, nt_b)
                    act(KT[cur][j][:, 0:T], bank[:, 0:T], AF.Copy, [bb], KT[cur].b(j))
                if out_tile and 'k' in osel:
                    kdst = ks if sample else kp[seq]
                    for gi, (g0, gm) in enumerate(groups):
                        bank, bb = gbank()
                        for k in range(16):
                            mm(bank[0:gm, 0:256], NT[k][:, g0:g0 + gm], wv[:, k, :], k == 0, k == 15,
                               [bw] + NT.b(k), [bb])
                        i = kvs_i[0] % 2
                        kvs_i[0] += 1
                        S.add("dve", lambda e, bank=bank, i=i, gm=gm: e.tensor_copy(
                            KVSt[i][0:gm, 0:256], bank[0:gm, 0:256]), reads=[bb], writes=[bKVS[i]])
                        dma("sp", kdst[g0:g0 + gm, blk * 256:(blk + 1) * 256], KVSt[i][0:gm, 0:256], [bKVS[i]], [bOUT])
            for blk in range(4):
                wv, bw = next_block(("std", w_in, 0, 16, 2048 + blk * 256, 256))
                for gi, (g0, gm) in enumerate(groups):
                    bank, bb = gbank()
                    for k in range(16):
                        mm(bank[0:gm, 0:256], NT[k][:, g0:g0 + gm], wv[:, k, :], k == 0, k == 15,
                           [bw] + NT.b(k), [bb])
                    S.add("dve", lambda e, bank=bank, gi=gi, gm=gm, blk=blk: e.tensor_copy(
                        VVt[cur][0:gm, gi, blk * 4:(blk + 1) * 4, 0:64],
                        bank[0:gm, 0:256].rearrange("p (h d) -> p h d", h=4)), reads=[bb], writes=[bVV[cur][gi]])
                    if out_tile and 'v' in osel:
                        vdst = vs if sample else vp[seq]
                        i = kvs_i[0] % 2
                        kvs_i[0] += 1
                        act(KVSt[i][0:gm, 0:256], bank[0:gm, 0:256], AF.Copy, [bb], [bKVS[i]])
                        dma("sp", vdst[g0:g0 + gm, blk * 256:(blk + 1) * 256], KVSt[i][0:gm, 0:256], [bKVS[i]], [bOUT])

            chk("qkv")
            if sample:
                dma("pool", COSt[:], sc, [], [bCOS])
                bank, bb = gbank()
                for j in range(8):
                    tr(bank[:, j * 8:(j + 1) * 8], COSt[0:8, j * 128:(j + 1) * 128], IDft[0:8, 0:8], [bCOS, bIDf], [bb])
                S.add("dve", lambda e, bank=bank: e.tensor_copy(COt[:].rearrange("p j r -> p (j r)"), bank[:, 0:64]),
                      reads=[bb], writes=[bCO])

            def uview(j):
                return Uc[j].rearrange("p (b l) -> p b l", b=nb)

            t3s = {}
            for jp in range(0, 8, 2):
                for j in (jp, jp + 1):
                    wv, bw = next_block(("conv", j))
                    bankc, bbc = gbank()
                    for k in range(16):
                        mm(bankc[:, 0:T], wv[:, k, 0, :], NT[k][:, 0:T], k == 0, k == 15, [bw] + NT.b(k), [bbc])
                    bankx, bbx = gbank()
                    for k in range(16):
                        mm(bankx[:, 0:T], wv[:, k, 1, :], NT[k][:, 0:T], k == 0, k == 15, [bw] + NT.b(k), [bbx])
                    ccs, bccs = tmp()
                    act(ccs[:, 0:T], bankc[:, 0:T], AF.Copy, [bbc], [bccs])
                    uv = uview(j)
                    if sample:
                        S.add("pool", lambda e, uv=uv, j=j: e.tensor_copy(
                            uv[:, :, 0:2], COt[:, j, :].rearrange("p (b r) -> p b r", b=4)),
                            reads=[bCO], writes=Uc.b(j))
                    elif t == 0:
                        S.add("pool", lambda e, uv=uv: e.memset(uv[:, :, 0:2], 0.0), writes=Uc.b(j))
                    else:
                        S.add("pool", lambda e, uv=uv, j=j: e.tensor_copy(uv[:, 0, 0:2], CARRYt[:, j, 0:2]),
                              reads=[bCARRY], writes=Uc.b(j))
                    S.add("dve", lambda e, uv=uv, ccs=ccs, bankx=bankx: e.tensor_tensor(
                        uv[:, :, 2:L + 2], ccs[:, 0:T].rearrange("p (b l) -> p b l", b=nb),
                        bankx[:, 0:T].rearrange("p (b l) -> p b l", b=nb), ALU.mult),
                        reads=[bccs, bbx] + Uc.b(j), writes=Uc.b(j))
                    if sample:
                        S.add("pool", lambda e, uv=uv, j=j: e.tensor_copy(
                            COt[:, j, :].rearrange("p (b r) -> p b r", b=4), uv[:, :, L:L + 2]),
                            reads=Uc.b(j) + [bCO], writes=[bCO])
                    else:
                        S.add("pool", lambda e, uv=uv, j=j: e.tensor_copy(CARRYt[:, j, 0:2], uv[:, 0, L:L + 2]),
                              reads=Uc.b(j) + [bCARRY], writes=[bCARRY])
                    t1, bt1 = tmp()
                    t1v = t1[:, 0:T].rearrange("p (b l) -> p b l", b=nb)
                    S.add("dve", lambda e, uv=uv, t1v=t1v, j=j: e.tensor_scalar(
                        t1v, uv[:, :, 2:L + 2], convw(2, j), convb(j), ALU.mult, ALU.add),
                        reads=Uc.b(j) + [bSMT], writes=[bt1])
                    S.add("dve", lambda e, uv=uv, t1v=t1v, j=j: e.scalar_tensor_tensor(
                        t1v, uv[:, :, 1:L + 1], convw(1, j), t1v, ALU.mult, ALU.add),
                        reads=Uc.b(j) + [bSMT, bt1], writes=[bt1])
                    S.add("dve", lambda e, uv=uv, t1v=t1v, j=j: e.scalar_tensor_tensor(
                        t1v, uv[:, :, 0:L], convw(0, j), t1v, ALU.mult, ALU.add),
                        reads=Uc.b(j) + [bSMT, bt1], writes=[bt1])
                    t3s[j] = (t1, bt1)
                wv, bw = next_block(("std", w_in, 0, 16, 3072 + jp * 128, 256))
                for jj in range(2):
                    j = jp + jj
                    bank, bb = proj_fm(T, wv, bw, 16, jj, nt_rhs, nt_b)
                    t3, bt3 = t3s[j]
                    S.add("dve", lambda e, bank=bank, t3=t3, j=j: e.tensor_tensor(
                        YBIN[j][:, 0:T], bank[:, 0:T], t3[:, 0:T], ALU.mult),
                        reads=[bb, bt3], writes=YBIN.b(j))
            if out_tile and 'c' in osel:
                nrow = 8 if sample else 2
                src_t = COt if sample else CARRYt
                bsrc = bCO if sample else bCARRY
                for half in range(2):
                    bank, bb = gbank()
                    for jj in range(4):
                        j = half * 4 + jj
                        tr(bank[0:8, jj * 128:(jj + 1) * 128], src_t[:, j, 0:8], IDft[:], [bsrc, bIDf], [bb])
                    S.add("dve", lambda e, bank=bank, half=half, nrow=nrow: e.tensor_copy(
                        COSt[0:nrow, half * 512:(half + 1) * 512], bank[0:nrow, :]), reads=[bb], writes=[bCOS])
                dma("pool", cs if sample else cp[seq], COSt[0:nrow, :], [bCOS], [bOUT])

            chk("conv")
            if not sample:
                LAG = 2

                def kv_of(s, j):
                    ksub = 4 * t + s - 4 + j
                    kt_tile, ksl = divmod(ksub, 4)
                    return kt_tile % 2, ksl

                def blocks_of(s):
                    return [j for j in range(5) if 4 * t + s - 4 + j >= 0]

                units = [(s, h, j) for s in range(NS) for h in range(NH) for j in blocks_of(s)]
                uidx = {u: i for i, u in enumerate(units)}
                nbk = (len(units) + 3) // 4
                upos = {}
                heads = [(s, h) for s in range(NS) for h in range(NH)]
                hp = 0
                tr_done = 0
                pv_emit_iter = {}
                pv_bank = {}

                def emit_pv(s, h):
                    blocks = blocks_of(s)
                    hi = h % 4
                    if hi == 0:
                        pv_bank[(s, h // 4)] = 6 + (pv_i[0] % 2)
                        pv_i[0] += 1
                    pvi = pv_bank[(s, h // 4)]
                    for bi_, j in enumerate(blocks):
                        ei, ui = upos[(s, h, j)]
                        bi, ksl = kv_of(s, j)
                        mm(PB[pvi][:, hi * 65:(hi + 1) * 65], Et[ei][:, ui * 128:(ui + 1) * 128],
                           VVt[bi][:, ksl, h, :], bi_ == 0, bi_ == len(blocks) - 1,
                           [bE[ei], bVV[bi][ksl]], [bPB[pvi]])
                    if hi == 3:
                        hg = h // 4
                        atk, batk = ATK[s % 2]
                        pvv = PB[pvi][:, 0:260].rearrange("p (h d) -> p h d", h=4)
                        S.add("dve", lambda e, pvv=pvv: e.reciprocal(
                            RDt[:].rearrange("p (h o) -> p h o", o=1), pvv[:, :, 64:65]), reads=[bPB[pvi]], writes=[bRD])
                        rdb = bass.AP(RDt[:].tensor, RDt[:].offset, [list(RDt[:].ap[0]), [1, 4], [0, HD]])
                        S.add("dve", lambda e, pvv=pvv, rdb=rdb, atk=atk, hg=hg: e.tensor_tensor(
                            atk[:, hg * 4:(hg + 1) * 4, :], pvv[:, :, 0:64], rdb, ALU.mult),
                            reads=[bPB[pvi], bRD], writes=batk)

                def emit_tr(s):
                    atk, batk = ATK[s % 2]
                    bank, bb = gbank()
                    bkb = bank[:].bitcast(BF16)
                    atf = atk.rearrange("p h d -> p (h d)")
                    for c in range(8):
                        tr(bkb[:, c * 128:(c + 1) * 128], atf[:, c * 128:(c + 1) * 128], IDbt[:], batk + [bIDb], [bb])
                    wb = []
                    for c in range(8):
                        wb += AOT.b(c)
                    S.add("dve", lambda e, bkb=bkb, s=s: e.tensor_copy(
                        AOT.ap[:, :, s * 128:(s + 1) * 128], bkb.rearrange("p (c t) -> p c t", c=8)),
                        reads=[bb], writes=wb)

                it = 0
                while tr_done < NS:
                    if it < nbk:
                        bank, bb = gbank()
                        us = units[it * 4:(it + 1) * 4]
                        for ui, (s, h, j) in enumerate(us):
                            ch, pr = h // 2, (h % 2) * 64
                            bi, ksl = kv_of(s, j)
                            outp = bank[:, ui * 128:(ui + 1) * 128]
                            need_b = j in (0, 3, 4)
                            mm(outp, KTt[bi][pr:pr + 64, ch, ksl * 128:(ksl + 1) * 128],
                               QT[ch][pr:pr + 64, s * 128:(s + 1) * 128], True, not need_b,
                               KT[bi].b(ch) + QT.b(ch), [bb])
                            if j == 0:
                                mm(outp, IDbt[:], M0t[:], False, True, [bIDb, bM0], [bb])
                            elif j >= 3:
                                mm(outp, IDbt[:], BTt[:, h, j - 3, :], False, True, [bIDb, bBT], [bb])
                        ei = e_i[0] % NE
                        e_i[0] += 1
                        n = len(us) * 128
                        act(Et[ei][:, 0:n], bank[:, 0:n], AF.Exp, [bb], [bE[ei]])
                        for ui, u in enumerate(us):
                            upos[u] = (ei, ui)
                    ready = 4 * (it - LAG + 1) if it < nbk + LAG - 1 else len(units)
                    while hp < len(heads):
                        s, h = heads[hp]
                        if uidx[(s, h, blocks_of(s)[-1])] >= ready:
                            break
                        emit_pv(s, h)
                        if h == NH - 1:
                            pv_emit_iter[s] = it
                        hp += 1
                    while tr_done < NS and tr_done in pv_emit_iter and (it >= pv_emit_iter[tr_done] + 2 or it >= nbk + LAG):
                        emit_tr(tr_done)
                        tr_done += 1
                    it += 1
            else:
                CKS = [(rview(lay["CKS"] + i * 4096, F32, [ATT]), rpg(lay["CKS"] + i * 4096, 4096)) for i in range(2)]
                CVS = [(rview(lay["CVS"] + i * 4096, F32, [ATT]), rpg(lay["CVS"] + i * 4096, 4096)) for i in range(2)]
                for b in range(4):
                    for j in range(4):
                        cks, bcks = CKS[j % 2]
                        cvs, bcvs = CVS[j % 2]
                        dma("sp", cks, ck[b, j * 128:(j + 1) * 128, :], [], bcks)
                        dma("sp", cvs, cv[b, j * 128:(j + 1) * 128, :], [], bcvs)
                        for cg in range(2):
                            bank, bb = gbank()
                            for ci in range(4):
                                c = cg * 4 + ci
                                tr(bank[:, ci * 128:(ci + 1) * 128], cks[:, c * 128:(c + 1) * 128], IDft[:],
                                   bcks + [bIDf], [bb])
                            wb = []
                            for ci in range(4):
                                wb += KT[prv].b(cg * 4 + ci)
                            S.add("dve", lambda e, bank=bank, cg=cg, j=j: e.tensor_copy(
                                KTt[prv][:, cg * 4:(cg + 1) * 4, j * 128:(j + 1) * 128],
                                bank[:].rearrange("p (c t) -> p c t", c=4)), reads=[bb], writes=wb)
                        S.add("pool", lambda e, cvs=cvs, j=j: e.tensor_copy(
                            VVt[prv][:, j, :, 0:64], cvs.rearrange("p (h d) -> p h d", h=NH)),
                            reads=bcvs, writes=[bVV[prv][j]])
                    atk, batk = ATK[b % 2]
                    q0 = 32 * b
                    for hg in range(6):
                        heads = list(range(hg * 3, min(hg * 3 + 3, NH)))
                        bank, bb = gbank()
                        for hi, h in enumerate(heads):
                            ch, pr = h // 2, (h % 2) * 64
                            for j in range(4):
                                outp = bank[:, hi * 160 + j * 32:hi * 160 + (j + 1) * 32]
                                mm(outp, KTt[prv][pr:pr + 64, ch, j * 128:(j + 1) * 128],
                                   QT[ch][pr:pr + 64, q0:q0 + 32], True, j != 3,
                                   KT[prv].b(ch) + QT.b(ch), [bb])
                                if j == 3:
                                    mm(outp, IDbt[:], BTt[:, h, 0, 0:32], False, True, [bIDb, bBT], [bb])
                            outp = bank[0:32, hi * 160 + 128:hi * 160 + 160]
                            mm(outp, KTt[cur][pr:pr + 64, ch, q0:q0 + 32], QT[ch][pr:pr + 64, q0:q0 + 32],
                               True, False, KT[cur].b(ch) + QT.b(ch), [bb])
                            mm(outp, IDbt[0:32, 0:32], BTt[0:32, h, 1, 0:32], False, True, [bIDb, bBT], [bb])
                        ei = e_i[0] % NE
                        e_i[0] += 1
                        n = len(heads) * 160
                        act(Et[ei][:, 0:n], bank[:, 0:n], AF.Exp, [bb], [bE[ei]])
                        pvi = 6 + (pv_i[0] % 2)
                        pv_i[0] += 1
                        for hi, h in enumerate(heads):
                            for j in range(4):
                                mm(PB[pvi][0:32, hi * 65:(hi + 1) * 65],
                                   Et[ei][:, hi * 160 + j * 32:hi * 160 + (j + 1) * 32],
                                   VVt[prv][:, j, h, :], j == 0, False, [bE[ei], bVV[prv][j]], [bPB[pvi]])
                            mm(PB[pvi][0:32, hi * 65:(hi + 1) * 65], Et[ei][0:32, hi * 160 + 128:hi * 160 + 160],
                               VVt[cur][0:32, b, h, :], False, True, [bE[ei], bVV[cur][b]], [bPB[pvi]])
                        nh_ = len(heads)
                        pvv = PB[pvi][0:32, 0:nh_ * 65].rearrange("p (h d) -> p h d", h=nh_)
                        S.add("dve", lambda e, pvv=pvv, nh_=nh_: e.reciprocal(RDt[0:32, 0:nh_].rearrange("p (h o) -> p h o", o=1), pvv[:, :, 64:65]),
                              reads=[bPB[pvi]], writes=[bRD])
                        rdap = RDt[0:32, 0:nh_]
                        rdb = bass.AP(rdap.tensor, rdap.offset, [list(rdap.ap[0]), [1, nh_], [0, HD]])
                        S.add("dve", lambda e, pvv=pvv, rdb=rdb, atk=atk, hg=hg, nh_=nh_: e.tensor_tensor(
                            atk[0:32, hg * 3:hg * 3 + nh_, :], pvv[:, :, 0:64], rdb, ALU.mult),
                            reads=[bPB[pvi], bRD], writes=batk)
                    bank, bb = gbank()
                    bkb = bank[:].bitcast(BF16)
                    atf = atk.rearrange("p h d -> p (h d)")
                    for c in range(8):
                        tr(bkb[:, c * 32:(c + 1) * 32], atf[0:32, c * 128:(c + 1) * 128], IDbt[0:32, 0:32],
                           batk + [bIDb], [bb])
                    wb = []
                    for c in range(8):
                        wb += AOT.b(c)
                    S.add("dve", lambda e, bkb=bkb, q0=q0: e.tensor_copy(
                        AOT.ap[:, :, q0:q0 + 32], bkb[:, 0:256].rearrange("p (c t) -> p c t", c=8)),
                        reads=[bb], writes=wb)

            chk("attn")
            for jp in range(0, 16, 2):
                TA, MA = {}, {}
                wv, bw = next_block(("std", w_in, 0, 16, 6144 + jp * 128, 256))
                for jj in range(2):
                    bank, bb = proj_fm(T, wv, bw, 16, jj, nt_rhs, nt_b)
                    ta, bta = tmp()
                    act(ta[:, 0:T], bank[:, 0:T], AF.Tanh, [bb], [bta], scale=0.5)
                    TA[jj] = (ta, bta)
                wv, bw = next_block(("std", w_ao, 0, 8, jp * 128, 256))
                for jj in range(2):
                    bank, bb = proj_fm(T, wv, bw, 8, jj, lambda k: AOT[k][:, 0:T], lambda k: AOT.b(k))
                    ta, bta = TA[jj]
                    ma, bma = tmp()
                    S.add("dve", lambda e, ma=ma, ta=ta, bank=bank: e.scalar_tensor_tensor(
                        ma[:, 0:T], ta[:, 0:T], 1.0, bank[:, 0:T], ALU.add, ALU.mult),
                        reads=[bta, bb], writes=[bma])
                    MA[jj] = (ma, bma)
                wv, bw = next_block(("std", w_in, 0, 16, 8192 + jp * 128, 256))
                for jj in range(2):
                    bank, bb = proj_fm(T, wv, bw, 16, jj, nt_rhs, nt_b)
                    ta, bta = tmp()
                    act(ta[:, 0:T], bank[:, 0:T], AF.Tanh, [bb], [bta], scale=0.5)
                    TA[jj] = (ta, bta)
                wv, bw = next_block(("std", w_co, 0, 8, jp * 128, 256))
                for jj in range(2):
                    j = jp + jj
                    bank, bb = proj_fm(T, wv, bw, 8, jj, lambda k: YBIN[k][:, 0:T], lambda k: YBIN.b(k))
                    ta, bta = TA[jj]
                    ma, bma = MA[jj]
                    mb, bmb = tmp()
                    S.add("dve", lambda e, mb=mb, ta=ta, bank=bank: e.scalar_tensor_tensor(
                        mb[:, 0:T], ta[:, 0:T], 1.0, bank[:, 0:T], ALU.add, ALU.mult),
                        reads=[bta, bb], writes=[bmb])
                    S.add("pool", lambda e, j=j, ma=ma, mb=mb: e.tensor_tensor(
                        MERGE[j][:, 0:T], ma[:, 0:T], mb[:, 0:T], ALU.add),
                        reads=[bma, bmb], writes=MERGE.b(j))

            chk("b6")
            for jp in range(0, 16, 2):
                wv, bw = next_block(("std", w_o, 0, 16, jp * 128, 256))
                for jj in range(2):
                    j = jp + jj
                    bank, bb = proj_fm(T, wv, bw, 16, jj, lambda k: MERGE[k][:, 0:T], lambda k: MERGE.b(k))
                    ss_add(T, j, bank[:, 0:T], [bb])
                    S.add("dve", lambda e, j=j, bank=bank: e.tensor_scalar(
                        MIX[j][:, 0:T], bank[:, 0:T], gain(1, j), 1.0, ALU.mult, ALU.mult),
                        reads=[bb, bSMT], writes=MIX.b(j))
            ss_finish(T, EPS)

            def after_mix(c):
                ss_add(T, c, XT[c][:, 0:T], XT.b(c))
                prescale_x(T, 2, c)
            residual_add(T, MIX, after_mix)

            chk("mix")
            ss_finish(T, EPS)
            finish_norm(T)
            for q in range(4):
                for jp in range(0, 16, 2):
                    wv, bw = next_block(("std", w_up, 0, 16, q * 2048 + jp * 128, 256))
                    for jj in range(2):
                        j = jp + jj
                        bank, bb = proj_fm(T, wv, bw, 16, jj, nt_rhs, nt_b)
                        sq, bsq = tmp()
                        act(sq[:, 0:T], bank[:, 0:T], AF.Square, [bb], [bsq])
                        S.add("dve", lambda e, j=j, bank=bank, sq=sq: e.scalar_tensor_tensor(
                            UPQ[j][:, 0:T], bank[:, 0:T], 0.0, sq[:, 0:T], ALU.is_gt, ALU.mult),
                            reads=[bb, bsq], writes=UPQ.b(j))
                for jp in range(0, 16, 2):
                    wv, bw = next_block(("std", w_dn, q * 2048, 16, jp * 128, 256))
                    for jj in range(2):
                        j = jp + jj
                        bank, bb = proj_fm(T, wv, bw, 16, jj, lambda k: UPQ[k][:, 0:T], lambda k: UPQ.b(k))
                        if q == 0:
                            act(Fc[j][:, 0:T], bank[:, 0:T], AF.Copy, [bb], Fc.b(j))
                        else:
                            S.add("dve", lambda e, j=j, bank=bank: e.tensor_tensor(
                                Fc[j][:, 0:T], Fc[j][:, 0:T], bank[:, 0:T], ALU.add),
                                reads=[bb] + Fc.b(j), writes=Fc.b(j))
                        if q == 3:
                            ss_add(T, j, Fc[j][:, 0:T], Fc.b(j))
                            S.add("dve", lambda e, j=j: e.tensor_scalar(
                                Fc[j][:, 0:T], Fc[j][:, 0:T], gain(3, j), 1.0, ALU.mult, ALU.mult),
                                reads=Fc.b(j) + [bSMT], writes=Fc.b(j))
            ss_finish(T, EPS)

            def after_f(c):
                act(NT[c][:, 0:T], XT[c][:, 0:T], AF.Copy, XT.b(c), NT.b(c))
            residual_add(T, Fc, after_f)

            chk("mlp")
            wvp, bwp = next_block(("std", w_pe, 0, 2, 0, 2048))
            projb = {}
            for j in range(DC):
                pass
            PRJ = rchunked(32768, BF16, DC, T)
            for j in range(DC):
                bank, bb = gbank()
                for k in range(2):
                    mm(bank[:, 0:T], wvp[:, k, j * 128:(j + 1) * 128], PTt[:, k, 0:T], k == 0, k == 1, [bwp, bPT], [bb])
                act(PRJ[j][:, 0:T], bank[:, 0:T], AF.Copy, [bb], PRJ.b(j))
            for jp in range(0, 16, 2):
                wv, bw = next_block(("std", w_pg, 0, 16, jp * 128, 256))
                for jj in range(2):
                    j = jp + jj
                    bank, bb = proj_fm(T, wv, bw, 16, jj, nt_rhs, nt_b)
                    ta, bta = tmp()
                    act(ta[:, 0:T], bank[:, 0:T], AF.Tanh, [bb], [bta], scale=0.5)
                    S.add("dve", lambda e, j=j, ta=ta: e.scalar_tensor_tensor(
                        Gc[j][:, 0:T], ta[:, 0:T], 1.0, PRJ[j][:, 0:T], ALU.add, ALU.mult),
                        reads=[bta] + PRJ.b(j), writes=Gc.b(j))
                    ss_add(T, j, Gc[j][:, 0:T], Gc.b(j))
                    S.add("dve", lambda e, j=j: e.tensor_scalar(
                        Gc[j][:, 0:T], Gc[j][:, 0:T], gain(4, j), 1.0, ALU.mult, ALU.mult),
                        reads=Gc.b(j) + [bSMT], writes=Gc.b(j))
            ss_finish(T, 4.0 * EPS)
            residual_add(T, Gc)

            chk("peg")
            if nxt is not None:
                nseq, nt_ = divmod(nxt, 4)
                nsrc = xp[nseq, nt_ * 512:nt_ * 512 + 128, :]
                for cg in range(4):
                    dma("sp", TMPt[cg][:, 0:512], nsrc[:, cg * 512:(cg + 1) * 512], [], [bTMP[cg]])
                xpre["on"] = True
            for s in range(NS):
                io, bio = IOST[s % 2]
                for cg in range(4):
                    bank, bb = gbank()
                    for ci in range(4):
                        c = cg * 4 + ci
                        tr(bank[:, ci * 128:(ci + 1) * 128], XT[c][:, s * 128:(s + 1) * 128], IDft[:],
                           XT.b(c) + [bIDf], [bb])
                    S.add("dve", lambda e, bank=bank, io=io, cg=cg: e.tensor_copy(
                        io[:, cg * 512:(cg + 1) * 512], bank[:]), reads=[bb], writes=bio)
                dma("sp", ydst[s * 128:(s + 1) * 128, :], io, bio, [bOUT])

        try:
            chk("setup")
            if do_sample:
                do_tile(None, True, nxt=(0 if n_prompt_tiles > 0 else None))
            for tix in range(n_prompt_tiles):
                do_tile(tix, False, nxt=(tix + 1 if tix + 1 < n_prompt_tiles else None))
        except _Stop:
            pass

        nep = S.finalize()
        esem = {e: [st.enter_context(nc.semaphore(f"es_{e}{i}")) for i in range(max(nep[e], 1))] for e in ENGS}
        dsem = {e: [st.enter_context(nc.semaphore(f"ds_{e}{i}")) for i in range(max(nep["d_" + e], 1))]
                for e in ("sp", "pool")}
        block = st.enter_context(nc.Block())
        S.emit(nc, block, esem, dsem)
    return nc


_NC_CACHE = {}


def _get_nc():
    if "nc" not in _NC_CACHE:
        _NC_CACHE["nc"] = build_program()
    return _NC_CACHE["nc"]


def kernel(x_prompt, x_sample, cache_k, cache_v, state_conv, p_prompt, p_sample,
           w_in, rel_table, w_attn_out, conv_w, conv_b, w_conv_out, w_o,
           g_pre_mix, g_post_mix, g_pre_mlp, g_post_mlp, w_up, w_down, w_pe, w_pe_gate, g_pe):
    f = lambda a: np.ascontiguousarray(np.asarray(a, dtype=np.float32))
    x_prompt, x_sample, cache_k, cache_v = f(x_prompt), f(x_sample), f(cache_k), f(cache_v)
    state_conv, p_prompt, p_sample = f(state_conv), f(p_prompt), f(p_sample)
    gains = np.stack([f(g_pre_mix)[0], f(g_post_mix)[0], f(g_pre_mlp)[0], f(g_post_mlp)[0], f(g_pe)[0]])
    shared = {
        "w_in": f(w_in)[0], "rel": f(rel_table)[0], "w_ao": f(w_attn_out)[0], "cw": f(conv_w)[0],
        "cb": f(conv_b), "w_co": f(w_conv_out)[0], "w_o": f(w_o)[0], "gains": gains,
        "w_up": f(w_up)[0], "w_dn": f(w_down)[0], "w_pe": f(w_pe)[0], "w_pg": f(w_pe_gate)[0],
    }
    in_maps = []
    for c in range(8):
        m = dict(shared)
        m["xp"] = x_prompt[2 * c:2 * c + 2]
        m["xs"] = x_sample[4 * c:4 * c + 4].reshape(128, D)
        m["ck"] = cache_k[0, 4 * c:4 * c + 4].reshape(4, 512, ATT)
        m["cv"] = cache_v[0, 4 * c:4 * c + 4].reshape(4, 512, ATT)
        m["sc"] = state_conv[0, 4 * c:4 * c + 4].reshape(8, CONV)
        m["pp"] = p_prompt[0, 2 * c:2 * c + 2]
        m["ps"] = p_sample[0, 4 * c:4 * c + 4].reshape(128, DPLE)
        in_maps.append(m)
    nc = _get_nc()
    res = run_bass_kernel_spmd(nc, in_maps, core_ids=list(range(8)))
    r = res.results
    cat = lambda k: np.concatenate([np.asarray(r[c][k]) for c in range(8)], axis=0)
    y_prompt = cat("yp")
    y_sample = cat("ys").reshape(32, 32, D)
    k_prompt = cat("kp").reshape(1, 16, 512, NH, HD)
    v_prompt = cat("vp").reshape(1, 16, 512, NH, HD)
    conv_prompt = cat("cp").reshape(1, 16, 2, CONV)
    k_sample = cat("ks").reshape(1, 32, 32, NH, HD)
    v_sample = cat("vs").reshape(1, 32, 32, NH, HD)
    conv_sample = cat("cs").reshape(1, 32, 2, CONV)
    return (y_prompt, y_sample, k_prompt, v_prompt, conv_prompt, k_sample, v_sample, conv_sample)
```

```python
import os as _os
import numpy as np
import concourse.bass as bass
import concourse.mybir as mybir
from concourse.bass_utils import run_bass_kernel_spmd
from contextlib import ExitStack

F32 = mybir.dt.float32
BF16 = mybir.dt.bfloat16
U8 = mybir.dt.uint8
AF = mybir.ActivationFunctionType
ALU = mybir.AluOpType

ENGS = ("pe", "act", "dve", "pool", "sp")
NDMA_SLOTS = 8


class Buf:
    __slots__ = ("name", "last_w", "rc", "rd", "excl")

    def __init__(self, name, excl=False):
        self.name = name
        self.excl = excl
        self.last_w = None
        self.rc = {}
        self.rd = []


class Op:
    __slots__ = ("eng", "fn", "cdeps", "ddeps", "sig", "signo", "dma", "dslot", "dcnt", "pos")

    def __init__(self, eng, fn, dma, pos):
        self.eng = eng
        self.fn = fn
        self.cdeps = {}
        self.ddeps = {}
        self.sig = False
        self.signo = 0
        self.dma = dma
        self.dslot = 0
        self.dcnt = 0
        self.pos = pos


EPOCH = 2048
DMA_GEN = 60


class Sched:
    def __init__(self):
        self.ops = {e: [] for e in ENGS}

    def add(self, eng, fn, reads=(), writes=(), dma=False):
        op = Op(eng, fn, dma, len(self.ops[eng]))
        if any(b.excl for b in reads):
            writes = list(writes) + [b for b in reads if b.excl and b not in writes]
            reads = [b for b in reads if not b.excl]

        def dep(d):
            if d is op:
                return
            if d.dma:
                op.ddeps[d] = 1
            else:
                cur = op.cdeps.get(d.eng)
                if cur is None or d.pos > cur.pos:
                    op.cdeps[d.eng] = d

        for b in reads:
            if b.last_w is not None:
                dep(b.last_w)
        for b in writes:
            if b.last_w is not None:
                dep(b.last_w)
            for r in b.rc.values():
                dep(r)
            for r in b.rd:
                dep(r)
        for b in writes:
            b.last_w = op
            b.rc = {}
            b.rd = []
        for b in reads:
            if b.last_w is not op:
                if dma:
                    b.rd.append(op)
                else:
                    b.rc[eng] = op
        self.ops[eng].append(op)
        return op

    @staticmethod
    def _needs_wait(op, dep):
        if dep.dma or op.dma:
            return True
        if dep.eng != op.eng:
            return True
        return op.eng != "pe"

    def finalize(self):
        for e in ENGS:
            for op in self.ops[e]:
                for d in list(op.cdeps.values()) + list(op.ddeps):
                    if self._needs_wait(op, d):
                        d.sig = True
        nep = {}
        for e in ENGS:
            n = 0
            nd = 0
            for op in self.ops[e]:
                if op.dma:
                    r = nd // NDMA_SLOTS
                    op.dslot = (r // DMA_GEN) * NDMA_SLOTS + nd % NDMA_SLOTS
                    op.dcnt = r % DMA_GEN + 1
                    nd += 1
                elif op.sig:
                    n += 1
                    op.signo = n
            nep[e] = (n + EPOCH - 1) // EPOCH
            nep["d_" + e] = ((nd + NDMA_SLOTS - 1) // NDMA_SLOTS + DMA_GEN - 1) // DMA_GEN * NDMA_SLOTS
        return nep

    def emit(self, nc, block, esem, dsem):
        engobj = {"pe": nc.tensor, "act": nc.scalar, "dve": nc.vector,
                  "pool": nc.gpsimd, "sp": nc.sync}
        deco = {"pe": block.tensor, "act": block.scalar, "dve": block.vector,
                "pool": block.gpsimd, "sp": block.sync}

        def run_engine(e):
            eng = engobj[e]
            waited = {}

            def wait(sem, val, key):
                if waited.get(key, 0) >= val:
                    return
                waited[key] = val
                eng.wait_ge(sem, val)

            def wait_c(d):
                ep, v = divmod(d.signo - 1, EPOCH)
                for k in list(waited):
                    pass
                wait(esem[d.eng][ep], v + 1, ("e", d.eng, ep))

            last_d = {}
            prev_in_lane = {}
            for op in self.ops[e]:
                for d in op.cdeps.values():
                    if self._needs_wait(op, d):
                        wait_c(d)
                for d in op.ddeps:
                    wait(dsem[d.eng][d.dslot], 16 * d.dcnt, ("d", d.eng, d.dslot))
                if op.dma:
                    prev = prev_in_lane.get(op.dslot % NDMA_SLOTS)
                    if prev is not None:
                        wait(dsem[e][prev.dslot], 16 * prev.dcnt, ("d", e, prev.dslot))
                    prev_in_lane[op.dslot % NDMA_SLOTS] = op
                    ins = op.fn(eng)
                    ins.then_inc(dsem[e][op.dslot], 16)
                    last_d[op.dslot] = op.dcnt
                else:
                    ins = op.fn(eng)
                    if op.sig:
                        ep = (op.signo - 1) // EPOCH
                        ins.then_inc(esem[e][ep], 1)
            for slot, cnt in last_d.items():
                wait(dsem[e][slot], 16 * cnt, ("d", e, slot))

        for e in ENGS:
            if not self.ops[e]:
                continue

            def _f(_eng, e=e):
                run_engine(e)
            deco[e](_f)


D = 2048
DC = 16
ATT = 1024
NH = 16
HD = 64
CONV = 1024
DFF = 8192
DPLE = 256
INC = 10240
EPS = 1e-6
NEG = -30000.0
NSLOT = 4
SLOT = 4096
RBYTES = 49152
NBLK = 153
TT_P = 512


def _prod(s):
    r = 1
    for v in s:
        r *= v
    return r


class Chunked:
    def __init__(self, ap, bufs):
        self.ap = ap
        self.bufs = bufs

    def __getitem__(self, c):
        return self.ap[:, c]

    def b(self, c):
        return self.bufs[c]

    def allb(self):
        out = []
        for x in self.bufs:
            for y in x:
                if y not in out:
                    out.append(y)
        return out


class _Stop(Exception):
    pass


def build_program(n_prompt_tiles=8, do_sample=True, stop=None):
    nc = bass.Bass("TRN2", target_bir_lowering=False)
    dtn = nc.dram_tensor

    def din(name, shape):
        return dtn(name, list(shape), F32, kind="ExternalInput").ap()

    def dout(name, shape):
        return dtn(name, list(shape), F32, kind="ExternalOutput").ap()

    xp = din("xp", [2, 2048, D])
    xs = din("xs", [128, D])
    ck = din("ck", [4, 512, ATT])
    cv = din("cv", [4, 512, ATT])
    sc = din("sc", [8, CONV])
    pp = din("pp", [2, 2048, DPLE])
    ps_ = din("ps", [128, DPLE])
    w_in = din("w_in", [D, INC])
    rel = din("rel", [NH, 257])
    w_ao = din("w_ao", [ATT, D])
    cw = din("cw", [3, CONV])
    cb_ = din("cb", [1, CONV])
    w_co = din("w_co", [CONV, D])
    w_o = din("w_o", [D, D])
    gains = din("gains", [5, D])
    w_up = din("w_up", [D, DFF])
    w_dn = din("w_dn", [DFF, D])
    w_pe = din("w_pe", [DPLE, D])
    w_pg = din("w_pg", [D, D])

    yp = dout("yp", [2, 2048, D])
    ys = dout("ys", [128, D])
    kp = dout("kp", [2, 512, ATT])
    vp = dout("vp", [2, 512, ATT])
    cp = dout("cp", [2, 2, CONV])
    ks = dout("ks", [128, ATT])
    vs = dout("vs", [128, ATT])
    cs = dout("cs", [8, CONV])

    wscr = dtn("wscr", [NBLK, 128, SLOT], BF16, kind="Internal").ap()
    text = dtn("text", [NH, 400], F32, kind="Internal").ap()

    S = Sched()
    st = ExitStack()
    with st:
        def sb(name, shape, dt):
            return st.enter_context(nc.sbuf_tensor(name, list(shape), dt))

        XTt = sb("XT", [128, DC, 512], F32)
        NTt = sb("NT", [128, DC, 512], BF16)
        KTt = [sb(f"KT{i}", [128, 8, 512], BF16) for i in range(2)]
        VVt = [sb(f"VV{i}", [128, 4, NH, 65], BF16) for i in range(2)]
        Rt = sb("R", [128, RBYTES], U8)
        WSt = [sb(f"WS{i}", [128, SLOT], BF16) for i in range(NSLOT)]
        KVSt = [sb(f"KVS{i}", [128, 512], F32) for i in range(2)]
        SQt = [sb(f"SQ{i}", [128, 512], BF16) for i in range(3)]
        RSt = sb("RS", [128, 512], F32)
        RSTDt = sb("RSTD", [128, 512], F32)
        NTMP = 5
        TMPt = [sb(f"TMP{i}", [128, 512], F32) for i in range(NTMP)]
        NE = 6
        Et = [sb(f"E{i}", [128, 512], BF16) for i in range(NE)]
        BTt = sb("BT", [128, NH, 2, 128], BF16)
        M0t = sb("M0", [128, 128], BF16)
        IDbt = sb("IDb", [128, 128], BF16)
        IDft = sb("IDf", [128, 128], F32)
        Jft = sb("Jf", [128, 128], F32)
        ONESt = sb("ONES", [128, 128], BF16)
        PTt = sb("PT", [128, 2, 512], BF16)
        PSTt = [sb(f"PST{i}", [128, DPLE], F32) for i in range(2)]
        SMt = sb("SM", [128, 128], F32)
        SMTt = sb("SMT", [128, 128], F32)
        CHt = sb("CH", [128, NH], F32)
        CARRYt = sb("CARRY", [128, 8, 8], F32)
        COt = sb("CO", [128, 8, 8], F32)
        COSt = sb("COS", [8, CONV], F32)
        RDt = sb("RD", [128, 4], F32)
        BPt = sb("BP", [128, 2, 128], F32)

        PB = [st.enter_context(nc.psum_tensor(f"PB{i}", [128, 512], F32)) for i in range(8)]
        bPB = [Buf(f"pb{i}", excl=True) for i in range(8)]


        def mk(name, n):
            return [[Buf(f"{name}{i}")] for i in range(n)]

        XT = Chunked(XTt[:], mk("xt", DC))
        NT = Chunked(NTt[:], mk("nt", DC))
        KT = [Chunked(KTt[i][:], mk(f"kt{i}_", 8)) for i in range(2)]
        bVV = [[Buf(f"vv{i}_{j}") for j in range(4)] for i in range(2)]
        rpages = [Buf(f"rp{i}") for i in range(RBYTES // 1024)]
        bWS = [Buf(f"ws{i}") for i in range(NSLOT)]
        bKVS = [Buf(f"kvs{i}") for i in range(2)]
        bSQ = [Buf(f"sq{i}") for i in range(3)]
        bRS, bRSTD = Buf("rs"), Buf("rstd")
        bTMP = [Buf(f"tmp{i}") for i in range(NTMP)]
        bE = [Buf(f"e{i}") for i in range(NE)]
        bBT, bM0, bIDb, bIDf, bJf, bONES = (Buf(n) for n in "bt m0 idb idf jf ones".split())
        bPT = Buf("pt")
        bPST = [Buf(f"pst{i}") for i in range(2)]
        bSM, bSMT, bCH, bCARRY, bCO, bCOS, bRD, bBP = (Buf(n) for n in "sm smt ch carry co cos rd bp".split())
        bTEXT = Buf("text")
        bWSCR = [Buf(f"wscr{i}") for i in range(NBLK)]
        bOUT = Buf("out")

        def rview(off, dt, shape):
            isz = 4 if dt == F32 else 2
            nbytes = _prod(shape) * isz
            assert off + nbytes <= RBYTES and off % 4 == 0, (off, nbytes)
            ap = Rt[:, off:off + nbytes].bitcast(dt)
            if len(shape) == 2:
                ap = ap.rearrange("p (a b) -> p a b", a=shape[0])
            elif len(shape) == 3:
                ap = ap.rearrange("p (a b c) -> p a b c", a=shape[0], b=shape[1])
            return ap

        def rpg(off, nbytes):
            return rpages[off // 1024:(off + nbytes + 1023) // 1024]

        def rchunked(off, dt, nchunk, celems):
            isz = 4 if dt == F32 else 2
            ap = rview(off, dt, [nchunk, celems])
            bufs = [rpg(off + c * celems * isz, celems * isz) for c in range(nchunk)]
            return Chunked(ap, bufs)

        tmp_i = [0]

        def tmp():
            i = tmp_i[0] % NTMP
            tmp_i[0] += 1
            return TMPt[i], bTMP[i]

        gb_i = [0]

        gb_wide = [True]
        GBW = [0, 1, 2, 3, 4, 6, 7]

        def gbank():
            lst = GBW if gb_wide[0] else GBW[:5]
            i = lst[gb_i[0] % len(lst)]
            gb_i[0] += 1
            return PB[i], bPB[i]

        sq_i = [0]
        e_i = [0]
        pv_i = [0]
        kvs_i = [0]

        def mm(out, lhsT, rhs, start, stop, reads, writes):
            S.add("pe", lambda e: e.matmul(out, lhsT=lhsT, rhs=rhs, start=start, stop=stop),
                  reads=reads, writes=writes)

        def tr(out, in_, ident, reads, writes):
            S.add("pe", lambda e: e.transpose(out, in_, ident), reads=reads, writes=writes)

        def act(out, in_, func, reads, writes, scale=1.0, bias=0.0):
            S.add("act", lambda e: e.activation(out, in_, func, bias=bias, scale=scale),
                  reads=reads, writes=writes)

        def dma(eng, out, in_, reads, writes, slow=False):
            if slow:
                S.add(eng, lambda e: e.dma_start(out=out, in_=in_, allow_slow_non_contiguous=True),
                      reads=reads, writes=writes, dma=True)
            else:
                S.add(eng, lambda e: e.dma_start(out=out, in_=in_), reads=reads, writes=writes, dma=True)

        S.add("pool", lambda e: e.memset(IDft[:], 0.0), writes=[bIDf])
        S.add("pool", lambda e: e.affine_select(IDft[:], IDft[:], [[-1, 128]], ALU.not_equal, 1.0,
                                                base=0, channel_multiplier=1), reads=[bIDf], writes=[bIDf])
        S.add("pool", lambda e: e.memset(Jft[:], 0.0), writes=[bJf])
        S.add("pool", lambda e: e.affine_select(Jft[:], Jft[:], [[1, 128]], ALU.not_equal, 1.0,
                                                base=-127, channel_multiplier=1), reads=[bJf], writes=[bJf])
        S.add("dve", lambda e: e.tensor_copy(IDbt[:], IDft[:]), reads=[bIDf], writes=[bIDb])
        S.add("pool", lambda e: e.memset(ONESt[:], 1.0 / D), writes=[bONES])
        S.add("pool", lambda e: e.memset(M0t[:], 0.0), writes=[bM0])
        S.add("pool", lambda e: e.memset(M0t[0:64, 64:128], NEG), reads=[bM0], writes=[bM0])
        for i in range(2):
            S.add("pool", lambda e, i=i: e.memset(VVt[i][:, :, :, 64:65], 2.0), writes=bVV[i])
        S.add("pool", lambda e: e.memset(CARRYt[:], 0.0), writes=[bCARRY])

        S.add("pool", lambda e: e.memset(SMt[:], 0.0), writes=[bSM])
        for n in range(5):
            dma("sp", SMt[n * 16:(n + 1) * 16, :], gains[n].rearrange("(c p) -> c p", p=128), [], [bSM])
        for j in range(3):
            dma("pool", SMt[80 + j * 8:88 + j * 8, :], cw[j].rearrange("(c p) -> c p", p=128), [], [bSM])
        dma("pool", SMt[104:112, :], cb_[0].rearrange("(c p) -> c p", p=128), [], [bSM])
        tr(PB[0][:, 0:128], SMt[:], IDft[:], [bSM, bIDf], [bPB[0]])
        S.add("dve", lambda e: e.tensor_copy(SMTt[:], PB[0][:, 0:128]), reads=[bPB[0]], writes=[bSMT])
        S.add("dve", lambda e: e.tensor_scalar(SMTt[:, 80:112], SMTt[:, 80:112], 0.5, 0.0, ALU.mult, ALU.add),
              reads=[bSMT], writes=[bSMT])

        def gain(n, c):
            return SMTt[:, n * 16 + c:n * 16 + c + 1]

        def convw(j, c):
            return SMTt[:, 80 + j * 8 + c:80 + j * 8 + c + 1]

        def convb(c):
            return SMTt[:, 104 + c:105 + c]

        S.add("pool", lambda e: e.memset(TMPt[0][:], 0.0), writes=[bTMP[0]])
        dma("sp", text[:, 0:128], TMPt[0][0:NH, 0:128], [bTMP[0]], [bTEXT])
        dma("sp", text[:, 128:385], rel, [], [bTEXT])
        dma("sp", CHt[:], bass.AP(rel.tensor, 0, [[0, 128], [257, NH]]), [], [bCH], slow=True)
        for h in range(NH):
            dma("sp", BPt[:], bass.AP(text.tensor, h * 400 + 1, [[1, 128], [128, 2], [1, 128]]), [bTEXT], [bBP])
            for blk in range(2):
                tr(PB[1][:, blk * 128:(blk + 1) * 128], BPt[:, blk, :], Jft[:], [bBP, bJf], [bPB[1]])
            tb, btb = TMPt[1 + (h % 2)], bTMP[1 + (h % 2)]
            S.add("dve", lambda e, h=h, tb=tb: e.tensor_scalar(
                tb[:, 0:256], PB[1][:, 0:256], CHt[:, h:h + 1], 1.0, ALU.subtract, ALU.mult),
                reads=[bPB[1], bCH], writes=[btb])
            S.add("pool", lambda e, tb=tb: e.affine_select(tb[:, 0:128], tb[:, 0:128], [[-1, 128]], ALU.is_ge, 0.0,
                                                           base=0, channel_multiplier=1), reads=[btb], writes=[btb])
            S.add("pool", lambda e, tb=tb: e.memset(tb[64:128, 128:192], NEG), reads=[btb], writes=[btb])
            S.add("dve", lambda e, h=h, tb=tb: e.tensor_copy(
                BTt[:, h, :, :], tb[:, 0:256].rearrange("p (b k) -> p b k", b=2)), reads=[btb], writes=[bBT])

        def chk(name):
            if stop == name:
                raise _Stop()

        wstate = {"g": 0, "issued": 0, "seq": None, "tiles": 0}

        def wsrc(desc):
            kind = desc[0]
            if kind == "std":
                _, W, r0, kc, c0, ncols = desc
                src = W[r0:r0 + kc * 128, c0:c0 + ncols].rearrange("(k p) f -> p k f", p=128)
                return src, [kc, ncols]
            if kind == "conv":
                j = desc[1]
                src = bass.AP(w_in.tensor, 4096 + 128 * j, [[INC, 128], [128 * INC, 16], [1024, 2], [1, 128]])
                return src, [16, 2, 128]
            raise ValueError(kind)

        def slot_view(si, shp):
            n = _prod(shp)
            ap = WSt[si][:, 0:n]
            if len(shp) == 2:
                return ap.rearrange("p (k f) -> p k f", k=shp[0])
            return ap.rearrange("p (k s f) -> p k s f", k=shp[0], s=shp[1])

        def issue_load(gidx):
            seq = wstate["seq"]
            ti, bi = divmod(gidx, len(seq))
            desc = seq[bi]
            si = gidx % NSLOT
            src, shp = wsrc(desc)
            n = _prod(shp)
            from_f32 = ti == 0 or (ti == 1 and bi % 2 == 0)
            writeback = (ti == 0 and bi % 2 == 1) or (ti == 1 and bi % 2 == 0) or wstate["tiles"] == 1
            if from_f32 and desc[0] == "conv":
                for seg in range(2):
                    srcs = bass.AP(w_in.tensor, 4096 + 128 * desc[1] + 1024 * seg,
                                   [[INC, 128], [128 * INC, 16], [1, 128]])
                    dma("pool", slot_view(si, shp)[:, :, seg, :], srcs, [], [bWS[si]])
            elif from_f32:
                dma("pool", slot_view(si, shp), src, [], [bWS[si]])
            else:
                dma("pool", WSt[si][:, 0:n], wscr[bi, :, 0:n], [bWSCR[bi]], [bWS[si]])
            if writeback and wstate["tiles"] > 1:
                dma("sp", wscr[bi, :, 0:n], WSt[si][:, 0:n], [bWS[si]], [bWSCR[bi]])

        def next_block(desc):
            seq = wstate["seq"]
            g = wstate["g"]
            assert seq[g % len(seq)] == desc, (g, seq[g % len(seq)], desc)
            total = len(seq) * wstate["tiles"]
            while wstate["issued"] < min(g + NSLOT, total):
                issue_load(wstate["issued"])
                wstate["issued"] += 1
            wstate["g"] = g + 1
            si = g % NSLOT
            _, shp = wsrc(desc)
            return slot_view(si, shp), bWS[si]

        def tile_wseq():
            seq = []
            for c0 in range(0, 1024, 256):
                seq.append(("std", w_in, 0, 16, c0, 256))
            for c0 in range(0, 1024, 256):
                seq.append(("std", w_in, 0, 16, 1024 + c0, 256))
            for c0 in range(0, 1024, 256):
                seq.append(("std", w_in, 0, 16, 2048 + c0, 256))
            for j in range(0, 8, 2):
                seq.append(("conv", j))
                seq.append(("conv", j + 1))
                seq.append(("std", w_in, 0, 16, 3072 + j * 128, 256))
            for j in range(0, 16, 2):
                seq.append(("std", w_in, 0, 16, 6144 + j * 128, 256))
                seq.append(("std", w_ao, 0, 8, j * 128, 256))
                seq.append(("std", w_in, 0, 16, 8192 + j * 128, 256))
                seq.append(("std", w_co, 0, 8, j * 128, 256))
            for j in range(0, 16, 2):
                seq.append(("std", w_o, 0, 16, j * 128, 256))
            for q in range(4):
                for j in range(0, 16, 2):
                    seq.append(("std", w_up, 0, 16, q * 2048 + j * 128, 256))
                for j in range(0, 16, 2):
                    seq.append(("std", w_dn, q * 2048, 16, j * 128, 256))
            seq.append(("std", w_pe, 0, 2, 0, 2048))
            for j in range(0, 16, 2):
                seq.append(("std", w_pg, 0, 16, j * 128, 256))
            return seq

        wstate["seq"] = tile_wseq()
        assert len(wstate["seq"]) <= NBLK, len(wstate["seq"])
        wstate["tiles"] = n_prompt_tiles + (1 if do_sample else 0)

        ss_pend = []
        SS_LAG = 2

        def ss_add(T, c, ap, bufs):
            i = sq_i[0] % 3
            sq_i[0] += 1
            act(SQt[i][:, 0:T], ap, AF.Square, bufs, [bSQ[i]])
            ss_pend.append((T, c, i))
            while len(ss_pend) > SS_LAG:
                ss_flush_one()

        def ss_flush_one():
            T, c, i = ss_pend.pop(0)
            mm(PB[5][:, 0:T], ONESt[:], SQt[i][:, 0:T], c == 0, c == DC - 1, [bONES, bSQ[i]], [bPB[5]])

        def ss_finish(T, eps):
            while ss_pend:
                ss_flush_one()
            act(RSt[:, 0:T], PB[5][:, 0:T], AF.Sqrt, [bPB[5]], [bRS], bias=eps)
            S.add("dve", lambda e: e.reciprocal(RSTDt[:, 0:T], RSt[:, 0:T]), reads=[bRS], writes=[bRSTD])

        def prescale_x(T, n, c):
            S.add("dve", lambda e, c=c: e.tensor_scalar(
                NT[c][:, 0:T], XT[c][:, 0:T], gain(n, c), 1.0, ALU.mult, ALU.mult),
                reads=XT.b(c) + [bSMT], writes=NT.b(c))

        def finish_norm(T):
            for c in range(DC):
                S.add("dve", lambda e, c=c: e.tensor_tensor(
                    NT[c][:, 0:T], NT[c][:, 0:T], RSTDt[:, 0:T], ALU.mult),
                    reads=NT.b(c) + [bRSTD], writes=NT.b(c))

        def residual_add(T, srcg, after=None):
            for c in range(DC):
                t, bt = tmp()
                S.add("dve", lambda e, c=c, t=t: e.tensor_tensor(
                    t[:, 0:T], srcg[c][:, 0:T], RSTDt[:, 0:T], ALU.mult),
                    reads=srcg.b(c) + [bRSTD], writes=[bt])
                S.add("dve" if c % 2 == 0 else "pool", lambda e, c=c, t=t: e.tensor_tensor(
                    XT[c][:, 0:T], XT[c][:, 0:T], t[:, 0:T], ALU.add),
                    reads=XT.b(c) + [bt], writes=XT.b(c))
                if after is not None:
                    after(c)

        def proj_fm(T, wv, bw, kc, jj, rhs_of, rhs_bufs_of):
            bank, bb = gbank()
            for k in range(kc):
                mm(bank[:, 0:T], wv[:, k, jj * 128:(jj + 1) * 128], rhs_of(k), k == 0, k == kc - 1,
                   [bw] + rhs_bufs_of(k), [bb])
            return bank, bb

        def do_tile(tix, sample):
            T = 128 if sample else 512
            nb, L = (4, 32) if sample else (1, 512)
            if sample:
                seq, t = None, None
                cur, prv = 1, 0
                groups = [(32 * b, 32) for b in range(4)]
                xsrc, psrc = xs, ps_
                ydst = ys
                lay = dict(QT=0, U=2048, YBIN=6400, AOT=8448, ATK=10496, CKS=16384, CVS=24576,
                           MERGE=32768, MIX=36864)
            else:
                seq, t = divmod(tix, 4)
                cur, prv = t % 2, (t + 1) % 2
                groups = [(128 * s, 128) for s in range(4)]
                xsrc, psrc = xp[seq, t * 512:(t + 1) * 512, :], pp[seq, t * 512:(t + 1) * 512, :]
                ydst = yp[seq, t * 512:(t + 1) * 512, :]
                lay = dict(QT=0, U=8192, YBIN=24704, AOT=32896, ATK=41088, MERGE=0, MIX=16384)
            out_tile = sample or (t == 3 and not _os.environ.get('K_NOOUT'))
            osel = 'kvc' if sample else _os.environ.get('K_OUTSEL', 'kvc')
            NS = T // 128
            UW = nb * (L + 2)

            QT = rchunked(lay["QT"], BF16, 8, T)
            Uc = rchunked(lay["U"], F32, 8, UW)
            YBIN = rchunked(lay["YBIN"], BF16, 8, T)
            AOT = rchunked(lay["AOT"], BF16, 8, T)
            ATK = [(rview(lay["ATK"] + i * 2048, BF16, [NH, HD]), rpg(lay["ATK"] + i * 2048, 2048)) for i in range(2)]
            MERGE = rchunked(lay["MERGE"], BF16, DC, T)
            MIX = rchunked(lay["MIX"], F32, DC, T)
            UPQ = rchunked(0, BF16, 16, T)
            Fc = rchunked(16384, F32, DC, T)
            Gc = rchunked(0, F32, DC, T)
            IOST = [(rview(32768 + i * 8192, F32, [D]), rpg(32768 + i * 8192, 8192)) for i in range(2)]

            for s in range(NS):
                io, bio = IOST[s % 2]
                dma("sp", io, xsrc[s * 128:(s + 1) * 128, :], [], bio)
                pst, bpst = PSTt[s % 2], bPST[s % 2]
                dma("sp", pst[:], psrc[s * 128:(s + 1) * 128, :], [], [bpst])
                for cg in range(4):
                    bank, bb = gbank()
                    for ci in range(4):
                        c = cg * 4 + ci
                        tr(bank[:, ci * 128:(ci + 1) * 128], io[:, c * 128:(c + 1) * 128], IDft[:],
                           bio + [bIDf], [bb])
                    wb = []
                    for ci in range(4):
                        wb += XT.b(cg * 4 + ci)
                    S.add("dve", lambda e, bank=bank, cg=cg, s=s: e.tensor_copy(
                        XTt[:, cg * 4:(cg + 1) * 4, s * 128:(s + 1) * 128],
                        bank[:].rearrange("p (c t) -> p c t", c=4)), reads=[bb], writes=wb)
                bank, bb = gbank()
                for c in range(2):
                    tr(bank[:, c * 128:(c + 1) * 128], pst[:, c * 128:(c + 1) * 128], IDft[:], [bpst, bIDf], [bb])
                S.add("dve", lambda e, bank=bank, s=s: e.tensor_copy(
                    PTt[:, :, s * 128:(s + 1) * 128], bank[:, 0:256].rearrange("p (c t) -> p c t", c=2)),
                    reads=[bb], writes=[bPT])

            chk("A")
            for c in range(DC):
                ss_add(T, c, XT[c][:, 0:T], XT.b(c))
                prescale_x(T, 0, c)
            ss_finish(T, EPS)
            finish_norm(T)

            def nt_rhs(k):
                return NT[k][:, 0:T]

            def nt_b(k):
                return NT.b(k)

            chk("norm1")
            for blk in range(4):
                wv, bw = next_block(("std", w_in, 0, 16, blk * 256, 256))
                for jj in range(2):
                    j = blk * 2 + jj
                    bank, bb = proj_fm(T, wv, bw, 16, jj, nt_rhs, nt_b)
                    act(QT[j][:, 0:T], bank[:, 0:T], AF.Copy, [bb], QT.b(j), scale=0.125)
            for blk in range(4):
                wv, bw = next_block(("std", w_in, 0, 16, 1024 + blk * 256, 256))
                for jj in range(2):
                    j = blk * 2 + jj
                    bank, bb = proj_fm(T, wv, bw, 16, jj, nt_rhs, nt_b)
                    act(KT[cur][j][:, 0:T], bank[:, 0:T], AF.Copy, [bb], KT[cur].b(j))
                if out_tile and 'k' in osel:
                    kdst = ks if sample else kp[seq]
                    for gi, (g0, gm) in enumerate(groups):
                        bank, bb = gbank()
                        for k in range(16):
                            mm(bank[0:gm, 0:256], NT[k][:, g0:g0 + gm], wv[:, k, :], k == 0, k == 15,
                               [bw] + NT.b(k), [bb])
                        i = kvs_i[0] % 2
                        kvs_i[0] += 1
                        S.add("dve", lambda e, bank=bank, i=i, gm=gm: e.tensor_copy(
                            KVSt[i][0:gm, 0:256], bank[0:gm, 0:256]), reads=[bb], writes=[bKVS[i]])
                        dma("sp", kdst[g0:g0 + gm, blk * 256:(blk + 1) * 256], KVSt[i][0:gm, 0:256], [bKVS[i]], [bOUT])
            for blk in range(4):
                wv, bw = next_block(("std", w_in, 0, 16, 2048 + blk * 256, 256))
                for gi, (g0, gm) in enumerate(groups):
                    bank, bb = gbank()
                    for k in range(16):
                        mm(bank[0:gm, 0:256], NT[k][:, g0:g0 + gm], wv[:, k, :], k == 0, k == 15,
                           [bw] + NT.b(k), [bb])
                    S.add("dve", lambda e, bank=bank, gi=gi, gm=gm, blk=blk: e.tensor_copy(
                        VVt[cur][0:gm, gi, blk * 4:(blk + 1) * 4, 0:64],
                        bank[0:gm, 0:256].rearrange("p (h d) -> p h d", h=4)), reads=[bb], writes=[bVV[cur][gi]])
                    if out_tile and 'v' in osel:
                        vdst = vs if sample else vp[seq]
                        i = kvs_i[0] % 2
                        kvs_i[0] += 1
                        act(KVSt[i][0:gm, 0:256], bank[0:gm, 0:256], AF.Copy, [bb], [bKVS[i]])
                        dma("sp", vdst[g0:g0 + gm, blk * 256:(blk + 1) * 256], KVSt[i][0:gm, 0:256], [bKVS[i]], [bOUT])

            chk("qkv")
            if sample:
                dma("pool", COSt[:], sc, [], [bCOS])
                bank, bb = gbank()
                for j in range(8):
                    tr(bank[:, j * 8:(j + 1) * 8], COSt[0:8, j * 128:(j + 1) * 128], IDft[0:8, 0:8], [bCOS, bIDf], [bb])
                S.add("dve", lambda e, bank=bank: e.tensor_copy(COt[:].rearrange("p j r -> p (j r)"), bank[:, 0:64]),
                      reads=[bb], writes=[bCO])

            def uview(j):
                return Uc[j].rearrange("p (b l) -> p b l", b=nb)

            t3s = {}
            for jp in range(0, 8, 2):
                for j in (jp, jp + 1):
                    wv, bw = next_block(("conv", j))
                    bankc, bbc = gbank()
                    for k in range(16):
                        mm(bankc[:, 0:T], wv[:, k, 0, :], NT[k][:, 0:T], k == 0, k == 15, [bw] + NT.b(k), [bbc])
                    bankx, bbx = gbank()
                    for k in range(16):
                        mm(bankx[:, 0:T], wv[:, k, 1, :], NT[k][:, 0:T], k == 0, k == 15, [bw] + NT.b(k), [bbx])
                    ccs, bccs = tmp()
                    act(ccs[:, 0:T], bankc[:, 0:T], AF.Copy, [bbc], [bccs])
                    uv = uview(j)
                    if sample:
                        S.add("pool", lambda e, uv=uv, j=j: e.tensor_copy(
                            uv[:, :, 0:2], COt[:, j, :].rearrange("p (b r) -> p b r", b=4)),
                            reads=[bCO], writes=Uc.b(j))
                    elif t == 0:
                        S.add("pool", lambda e, uv=uv: e.memset(uv[:, :, 0:2], 0.0), writes=Uc.b(j))
                    else:
                        S.add("pool", lambda e, uv=uv, j=j: e.tensor_copy(uv[:, 0, 0:2], CARRYt[:, j, 0:2]),
                              reads=[bCARRY], writes=Uc.b(j))
                    S.add("dve", lambda e, uv=uv, ccs=ccs, bankx=bankx: e.tensor_tensor(
                        uv[:, :, 2:L + 2], ccs[:, 0:T].rearrange("p (b l) -> p b l", b=nb),
                        bankx[:, 0:T].rearrange("p (b l) -> p b l", b=nb), ALU.mult),
                        reads=[bccs, bbx] + Uc.b(j), writes=Uc.b(j))
                    if sample:
                        S.add("pool", lambda e, uv=uv, j=j: e.tensor_copy(
                            COt[:, j, :].rearrange("p (b r) -> p b r", b=4), uv[:, :, L:L + 2]),
                            reads=Uc.b(j) + [bCO], writes=[bCO])
                    else:
                        S.add("pool", lambda e, uv=uv, j=j: e.tensor_copy(CARRYt[:, j, 0:2], uv[:, 0, L:L + 2]),
                              reads=Uc.b(j) + [bCARRY], writes=[bCARRY])
                    t1, bt1 = tmp()
                    t1v = t1[:, 0:T].rearrange("p (b l) -> p b l", b=nb)
                    S.add("dve", lambda e, uv=uv, t1v=t1v, j=j: e.tensor_scalar(
                        t1v, uv[:, :, 2:L + 2], convw(2, j), convb(j), ALU.mult, ALU.add),
                        reads=Uc.b(j) + [bSMT], writes=[bt1])
                    S.add("dve", lambda e, uv=uv, t1v=t1v, j=j: e.scalar_tensor_tensor(
                        t1v, uv[:, :, 1:L + 1], convw(1, j), t1v, ALU.mult, ALU.add),
                        reads=Uc.b(j) + [bSMT, bt1], writes=[bt1])
                    S.add("dve", lambda e, uv=uv, t1v=t1v, j=j: e.scalar_tensor_tensor(
                        t1v, uv[:, :, 0:L], convw(0, j), t1v, ALU.mult, ALU.add),
                        reads=Uc.b(j) + [bSMT, bt1], writes=[bt1])
                    t3s[j] = (t1, bt1)
                wv, bw = next_block(("std", w_in, 0, 16, 3072 + jp * 128, 256))
                for jj in range(2):
                    j = jp + jj
                    bank, bb = proj_fm(T, wv, bw, 16, jj, nt_rhs, nt_b)
                    t3, bt3 = t3s[j]
                    S.add("dve", lambda e, bank=bank, t3=t3, j=j: e.tensor_tensor(
                        YBIN[j][:, 0:T], bank[:, 0:T], t3[:, 0:T], ALU.mult),
                        reads=[bb, bt3], writes=YBIN.b(j))
            if out_tile and 'c' in osel:
                nrow = 8 if sample else 2
                src_t = COt if sample else CARRYt
                bsrc = bCO if sample else bCARRY
                for half in range(2):
                    bank, bb = gbank()
                    for jj in range(4):
                        j = half * 4 + jj
                        tr(bank[0:8, jj * 128:(jj + 1) * 128], src_t[:, j, 0:8], IDft[:], [bsrc, bIDf], [bb])
                    S.add("dve", lambda e, bank=bank, half=half, nrow=nrow: e.tensor_copy(
                        COSt[0:nrow, half * 512:(half + 1) * 512], bank[0:nrow, :]), reads=[bb], writes=[bCOS])
                dma("pool", cs if sample else cp[seq], COSt[0:nrow, :], [bCOS], [bOUT])

            chk("conv")
            gb_wide[0] = False
            if not sample:
                LAG = 2

                def kv_of(s, j):
                    ksub = 4 * t + s - 4 + j
                    kt_tile, ksl = divmod(ksub, 4)
                    return kt_tile % 2, ksl

                def blocks_of(s):
                    return [j for j in range(5) if 4 * t + s - 4 + j >= 0]

                units = [(s, h, j) for s in range(NS) for h in range(NH) for j in blocks_of(s)]
                uidx = {u: i for i, u in enumerate(units)}
                nbk = (len(units) + 3) // 4
                upos = {}
                heads = [(s, h) for s in range(NS) for h in range(NH)]
                hp = 0
                tr_done = 0
                pv_emit_iter = {}
                pv_bank = {}

                def emit_pv(s, h):
                    blocks = blocks_of(s)
                    hi = h % 4
                    if hi == 0:
                        pv_bank[(s, h // 4)] = 6 + (pv_i[0] % 2)
                        pv_i[0] += 1
                    pvi = pv_bank[(s, h // 4)]
                    for bi_, j in enumerate(blocks):
                        ei, ui = upos[(s, h, j)]
                        bi, ksl = kv_of(s, j)
                        mm(PB[pvi][:, hi * 65:(hi + 1) * 65], Et[ei][:, ui * 128:(ui + 1) * 128],
                           VVt[bi][:, ksl, h, :], bi_ == 0, bi_ == len(blocks) - 1,
                           [bE[ei], bVV[bi][ksl]], [bPB[pvi]])
                    if hi == 3:
                        hg = h // 4
                        atk, batk = ATK[s % 2]
                        pvv = PB[pvi][:, 0:260].rearrange("p (h d) -> p h d", h=4)
                        S.add("dve", lambda e, pvv=pvv: e.reciprocal(
                            RDt[:].rearrange("p (h o) -> p h o", o=1), pvv[:, :, 64:65]), reads=[bPB[pvi]], writes=[bRD])
                        rdb = bass.AP(RDt[:].tensor, RDt[:].offset, [list(RDt[:].ap[0]), [1, 4], [0, HD]])
                        S.add("dve", lambda e, pvv=pvv, rdb=rdb, atk=atk, hg=hg: e.tensor_tensor(
                            atk[:, hg * 4:(hg + 1) * 4, :], pvv[:, :, 0:64], rdb, ALU.mult),
                            reads=[bPB[pvi], bRD], writes=batk)

                def emit_tr(s):
                    atk, batk = ATK[s % 2]
                    bank, bb = gbank()
                    bkb = bank[:].bitcast(BF16)
                    atf = atk.rearrange("p h d -> p (h d)")
                    for c in range(8):
                        tr(bkb[:, c * 128:(c + 1) * 128], atf[:, c * 128:(c + 1) * 128], IDbt[:], batk + [bIDb], [bb])
                    wb = []
                    for c in range(8):
                        wb += AOT.b(c)
                    S.add("dve", lambda e, bkb=bkb, s=s: e.tensor_copy(
                        AOT.ap[:, :, s * 128:(s + 1) * 128], bkb.rearrange("p (c t) -> p c t", c=8)),
                        reads=[bb], writes=wb)

                it = 0
                while tr_done < NS:
                    if it < nbk:
                        bank, bb = gbank()
                        us = units[it * 4:(it + 1) * 4]
                        for ui, (s, h, j) in enumerate(us):
                            ch, pr = h // 2, (h % 2) * 64
                            bi, ksl = kv_of(s, j)
                            outp = bank[:, ui * 128:(ui + 1) * 128]
                            need_b = j in (0, 3, 4)
                            mm(outp, KTt[bi][pr:pr + 64, ch, ksl * 128:(ksl + 1) * 128],
                               QT[ch][pr:pr + 64, s * 128:(s + 1) * 128], True, not need_b,
                               KT[bi].b(ch) + QT.b(ch), [bb])
                            if j == 0:
                                mm(outp, IDbt[:], M0t[:], False, True, [bIDb, bM0], [bb])
                            elif j >= 3:
                                mm(outp, IDbt[:], BTt[:, h, j - 3, :], False, True, [bIDb, bBT], [bb])
                        ei = e_i[0] % NE
                        e_i[0] += 1
                        n = len(us) * 128
                        act(Et[ei][:, 0:n], bank[:, 0:n], AF.Exp, [bb], [bE[ei]])
                        for ui, u in enumerate(us):
                            upos[u] = (ei, ui)
                    ready = 4 * (it - LAG + 1) if it < nbk + LAG - 1 else len(units)
                    while hp < len(heads):
                        s, h = heads[hp]
                        if uidx[(s, h, blocks_of(s)[-1])] >= ready:
                            break
                        emit_pv(s, h)
                        if h == NH - 1:
                            pv_emit_iter[s] = it
                        hp += 1
                    while tr_done < NS and tr_done in pv_emit_iter and (it >= pv_emit_iter[tr_done] + 2 or it >= nbk + LAG):
                        emit_tr(tr_done)
                        tr_done += 1
                    it += 1
            else:
                CKS = [(rview(lay["CKS"] + i * 4096, F32, [ATT]), rpg(lay["CKS"] + i * 4096, 4096)) for i in range(2)]
                CVS = [(rview(lay["CVS"] + i * 4096, F32, [ATT]), rpg(lay["CVS"] + i * 4096, 4096)) for i in range(2)]
                for b in range(4):
                    for j in range(4):
                        cks, bcks = CKS[j % 2]
                        cvs, bcvs = CVS[j % 2]
                        dma("sp", cks, ck[b, j * 128:(j + 1) * 128, :], [], bcks)
                        dma("sp", cvs, cv[b, j * 128:(j + 1) * 128, :], [], bcvs)
                        for cg in range(2):
                            bank, bb = gbank()
                            for ci in range(4):
                                c = cg * 4 + ci
                                tr(bank[:, ci * 128:(ci + 1) * 128], cks[:, c * 128:(c + 1) * 128], IDft[:],
                                   bcks + [bIDf], [bb])
                            wb = []
                            for ci in range(4):
                                wb += KT[prv].b(cg * 4 + ci)
                            S.add("dve", lambda e, bank=bank, cg=cg, j=j: e.tensor_copy(
                                KTt[prv][:, cg * 4:(cg + 1) * 4, j * 128:(j + 1) * 128],
                                bank[:].rearrange("p (c t) -> p c t", c=4)), reads=[bb], writes=wb)
                        S.add("pool", lambda e, cvs=cvs, j=j: e.tensor_copy(
                            VVt[prv][:, j, :, 0:64], cvs.rearrange("p (h d) -> p h d", h=NH)),
                            reads=bcvs, writes=[bVV[prv][j]])
                    atk, batk = ATK[b % 2]
                    q0 = 32 * b
                    for hg in range(6):
                        heads = list(range(hg * 3, min(hg * 3 + 3, NH)))
                        bank, bb = gbank()
                        for hi, h in enumerate(heads):
                            ch, pr = h // 2, (h % 2) * 64
                            for j in range(4):
                                outp = bank[:, hi * 160 + j * 32:hi * 160 + (j + 1) * 32]
                                mm(outp, KTt[prv][pr:pr + 64, ch, j * 128:(j + 1) * 128],
                                   QT[ch][pr:pr + 64, q0:q0 + 32], True, j != 3,
                                   KT[prv].b(ch) + QT.b(ch), [bb])
                                if j == 3:
                                    mm(outp, IDbt[:], BTt[:, h, 0, 0:32], False, True, [bIDb, bBT], [bb])
                            outp = bank[0:32, hi * 160 + 128:hi * 160 + 160]
                            mm(outp, KTt[cur][pr:pr + 64, ch, q0:q0 + 32], QT[ch][pr:pr + 64, q0:q0 + 32],
                               True, False, KT[cur].b(ch) + QT.b(ch), [bb])
                            mm(outp, IDbt[0:32, 0:32], BTt[0:32, h, 1, 0:32], False, True, [bIDb, bBT], [bb])
                        ei = e_i[0] % NE
                        e_i[0] += 1
                        n = len(heads) * 160
                        act(Et[ei][:, 0:n], bank[:, 0:n], AF.Exp, [bb], [bE[ei]])
                        pvi = 6 + (pv_i[0] % 2)
                        pv_i[0] += 1
                        for hi, h in enumerate(heads):
                            for j in range(4):
                                mm(PB[pvi][0:32, hi * 65:(hi + 1) * 65],
                                   Et[ei][:, hi * 160 + j * 32:hi * 160 + (j + 1) * 32],
                                   VVt[prv][:, j, h, :], j == 0, False, [bE[ei], bVV[prv][j]], [bPB[pvi]])
                            mm(PB[pvi][0:32, hi * 65:(hi + 1) * 65], Et[ei][0:32, hi * 160 + 128:hi * 160 + 160],
                               VVt[cur][0:32, b, h, :], False, True, [bE[ei], bVV[cur][b]], [bPB[pvi]])
                        nh_ = len(heads)
                        pvv = PB[pvi][0:32, 0:nh_ * 65].rearrange("p (h d) -> p h d", h=nh_)
                        S.add("dve", lambda e, pvv=pvv, nh_=nh_: e.reciprocal(RDt[0:32, 0:nh_].rearrange("p (h o) -> p h o", o=1), pvv[:, :, 64:65]),
                              reads=[bPB[pvi]], writes=[bRD])
                        rdap = RDt[0:32, 0:nh_]
                        rdb = bass.AP(rdap.tensor, rdap.offset, [list(rdap.ap[0]), [1, nh_], [0, HD]])
                        S.add("dve", lambda e, pvv=pvv, rdb=rdb, atk=atk, hg=hg, nh_=nh_: e.tensor_tensor(
                            atk[0:32, hg * 3:hg * 3 + nh_, :], pvv[:, :, 0:64], rdb, ALU.mult),
                            reads=[bPB[pvi], bRD], writes=batk)
                    bank, bb = gbank()
                    bkb = bank[:].bitcast(BF16)
                    atf = atk.rearrange("p h d -> p (h d)")
                    for c in range(8):
                        tr(bkb[:, c * 32:(c + 1) * 32], atf[0:32, c * 128:(c + 1) * 128], IDbt[0:32, 0:32],
                           batk + [bIDb], [bb])
                    wb = []
                    for c in range(8):
                        wb += AOT.b(c)
                    S.add("dve", lambda e, bkb=bkb, q0=q0: e.tensor_copy(
                        AOT.ap[:, :, q0:q0 + 32], bkb[:, 0:256].rearrange("p (c t) -> p c t", c=8)),
                        reads=[bb], writes=wb)

            chk("attn")
            gb_wide[0] = True
            for jp in range(0, 16, 2):
                TA, MA = {}, {}
                wv, bw = next_block(("std", w_in, 0, 16, 6144 + jp * 128, 256))
                for jj in range(2):
                    bank, bb = proj_fm(T, wv, bw, 16, jj, nt_rhs, nt_b)
                    ta, bta = tmp()
                    act(ta[:, 0:T], bank[:, 0:T], AF.Tanh, [bb], [bta], scale=0.5)
                    TA[jj] = (ta, bta)
                wv, bw = next_block(("std", w_ao, 0, 8, jp * 128, 256))
                for jj in range(2):
                    bank, bb = proj_fm(T, wv, bw, 8, jj, lambda k: AOT[k][:, 0:T], lambda k: AOT.b(k))
                    ta, bta = TA[jj]
                    ma, bma = tmp()
                    S.add("dve", lambda e, ma=ma, ta=ta, bank=bank: e.scalar_tensor_tensor(
                        ma[:, 0:T], ta[:, 0:T], 1.0, bank[:, 0:T], ALU.add, ALU.mult),
                        reads=[bta, bb], writes=[bma])
                    MA[jj] = (ma, bma)
                wv, bw = next_block(("std", w_in, 0, 16, 8192 + jp * 128, 256))
                for jj in range(2):
                    bank, bb = proj_fm(T, wv, bw, 16, jj, nt_rhs, nt_b)
                    ta, bta = tmp()
                    act(ta[:, 0:T], bank[:, 0:T], AF.Tanh, [bb], [bta], scale=0.5)
                    TA[jj] = (ta, bta)
                wv, bw = next_block(("std", w_co, 0, 8, jp * 128, 256))
                for jj in range(2):
                    j = jp + jj
                    bank, bb = proj_fm(T, wv, bw, 8, jj, lambda k: YBIN[k][:, 0:T], lambda k: YBIN.b(k))
                    ta, bta = TA[jj]
                    ma, bma = MA[jj]
                    mb, bmb = tmp()
                    S.add("dve", lambda e, mb=mb, ta=ta, bank=bank: e.scalar_tensor_tensor(
                        mb[:, 0:T], ta[:, 0:T], 1.0, bank[:, 0:T], ALU.add, ALU.mult),
                        reads=[bta, bb], writes=[bmb])
                    S.add("pool", lambda e, j=j, ma=ma, mb=mb: e.tensor_tensor(
                        MERGE[j][:, 0:T], ma[:, 0:T], mb[:, 0:T], ALU.add),
                        reads=[bma, bmb], writes=MERGE.b(j))

            chk("b6")
            for jp in range(0, 16, 2):
                wv, bw = next_block(("std", w_o, 0, 16, jp * 128, 256))
                for jj in range(2):
                    j = jp + jj
                    bank, bb = proj_fm(T, wv, bw, 16, jj, lambda k: MERGE[k][:, 0:T], lambda k: MERGE.b(k))
                    ss_add(T, j, bank[:, 0:T], [bb])
                    S.add("dve", lambda e, j=j, bank=bank: e.tensor_scalar(
                        MIX[j][:, 0:T], bank[:, 0:T], gain(1, j), 1.0, ALU.mult, ALU.mult),
                        reads=[bb, bSMT], writes=MIX.b(j))
            ss_finish(T, EPS)

            def after_mix(c):
                ss_add(T, c, XT[c][:, 0:T], XT.b(c))
                prescale_x(T, 2, c)
            residual_add(T, MIX, after_mix)

            chk("mix")
            ss_finish(T, EPS)
            finish_norm(T)
            for q in range(4):
                for jp in range(0, 16, 2):
                    wv, bw = next_block(("std", w_up, 0, 16, q * 2048 + jp * 128, 256))
                    for jj in range(2):
                        j = jp + jj
                        bank, bb = proj_fm(T, wv, bw, 16, jj, nt_rhs, nt_b)
                        sq, bsq = tmp()
                        act(sq[:, 0:T], bank[:, 0:T], AF.Square, [bb], [bsq])
                        S.add("dve", lambda e, j=j, bank=bank, sq=sq: e.scalar_tensor_tensor(
                            UPQ[j][:, 0:T], bank[:, 0:T], 0.0, sq[:, 0:T], ALU.is_gt, ALU.mult),
                            reads=[bb, bsq], writes=UPQ.b(j))
                for jp in range(0, 16, 2):
                    wv, bw = next_block(("std", w_dn, q * 2048, 16, jp * 128, 256))
                    for jj in range(2):
                        j = jp + jj
                        bank, bb = proj_fm(T, wv, bw, 16, jj, lambda k: UPQ[k][:, 0:T], lambda k: UPQ.b(k))
                        if q == 0:
                            act(Fc[j][:, 0:T], bank[:, 0:T], AF.Copy, [bb], Fc.b(j))
                        else:
                            S.add("dve", lambda e, j=j, bank=bank: e.tensor_tensor(
                                Fc[j][:, 0:T], Fc[j][:, 0:T], bank[:, 0:T], ALU.add),
                                reads=[bb] + Fc.b(j), writes=Fc.b(j))
                        if q == 3:
                            ss_add(T, j, Fc[j][:, 0:T], Fc.b(j))
                            S.add("dve", lambda e, j=j: e.tensor_scalar(
                                Fc[j][:, 0:T], Fc[j][:, 0:T], gain(3, j), 1.0, ALU.mult, ALU.mult),
                                reads=Fc.b(j) + [bSMT], writes=Fc.b(j))
            ss_finish(T, EPS)

            def after_f(c):
                act(NT[c][:, 0:T], XT[c][:, 0:T], AF.Copy, XT.b(c), NT.b(c))
            residual_add(T, Fc, after_f)

            chk("mlp")
            wvp, bwp = next_block(("std", w_pe, 0, 2, 0, 2048))
            projb = {}
            for j in range(DC):
                pass
            PRJ = rchunked(32768, BF16, DC, T)
            for j in range(DC):
                bank, bb = gbank()
                for k in range(2):
                    mm(bank[:, 0:T], wvp[:, k, j * 128:(j + 1) * 128], PTt[:, k, 0:T], k == 0, k == 1, [bwp, bPT], [bb])
                act(PRJ[j][:, 0:T], bank[:, 0:T], AF.Copy, [bb], PRJ.b(j))
            for jp in range(0, 16, 2):
                wv, bw = next_block(("std", w_pg, 0, 16, jp * 128, 256))
                for jj in range(2):
                    j = jp + jj
                    bank, bb = proj_fm(T, wv, bw, 16, jj, nt_rhs, nt_b)
                    ta, bta = tmp()
                    act(ta[:, 0:T], bank[:, 0:T], AF.Tanh, [bb], [bta], scale=0.5)
                    S.add("dve", lambda e, j=j, ta=ta: e.scalar_tensor_tensor(
                        Gc[j][:, 0:T], ta[:, 0:T], 1.0, PRJ[j][:, 0:T], ALU.add, ALU.mult),
                        reads=[bta] + PRJ.b(j), writes=Gc.b(j))
                    ss_add(T, j, Gc[j][:, 0:T], Gc.b(j))
                    S.add("dve", lambda e, j=j: e.tensor_scalar(
                        Gc[j][:, 0:T], Gc[j][:, 0:T], gain(4, j), 1.0, ALU.mult, ALU.mult),
                        reads=Gc.b(j) + [bSMT], writes=Gc.b(j))
            ss_finish(T, 4.0 * EPS)
            residual_add(T, Gc)

            chk("peg")
            for s in range(NS):
                io, bio = IOST[s % 2]
                for cg in range(4):
                    bank, bb = gbank()
                    for ci in range(4):
                        c = cg * 4 + ci
                        tr(bank[:, ci * 128:(ci + 1) * 128], XT[c][:, s * 128:(s + 1) * 128], IDft[:],
                           XT.b(c) + [bIDf], [bb])
                    S.add("dve", lambda e, bank=bank, io=io, cg=cg: e.tensor_copy(
                        io[:, cg * 512:(cg + 1) * 512], bank[:]), reads=[bb], writes=bio)
                dma("sp", ydst[s * 128:(s + 1) * 128, :], io, bio, [bOUT])

        try:
            chk("setup")
            if do_sample:
                do_tile(None, True)
            for tix in range(n_prompt_tiles):
                do_tile(tix, False)
        except _Stop:
            pass

        nep = S.finalize()
        esem = {e: [st.enter_context(nc.semaphore(f"es_{e}{i}")) for i in range(max(nep[e], 1))] for e in ENGS}
        dsem = {e: [st.enter_context(nc.semaphore(f"ds_{e}{i}")) for i in range(max(nep["d_" + e], 1))]
                for e in ("sp", "pool")}
        block = st.enter_context(nc.Block())
        S.emit(nc, block, esem, dsem)
    return nc


_NC_CACHE = {}


def _get_nc():
    if "nc" not in _NC_CACHE:
        _NC_CACHE["nc"] = build_program()
    return _NC_CACHE["nc"]


def kernel(x_prompt, x_sample, cache_k, cache_v, state_conv, p_prompt, p_sample,
           w_in, rel_table, w_attn_out, conv_w, conv_b, w_conv_out, w_o,
           g_pre_mix, g_post_mix, g_pre_mlp, g_post_mlp, w_up, w_down, w_pe, w_pe_gate, g_pe):
    f = lambda a: np.ascontiguousarray(np.asarray(a, dtype=np.float32))
    x_prompt, x_sample, cache_k, cache_v = f(x_prompt), f(x_sample), f(cache_k), f(cache_v)
    state_conv, p_prompt, p_sample = f(state_conv), f(p_prompt), f(p_sample)
    gains = np.stack([f(g_pre_mix)[0], f(g_post_mix)[0], f(g_pre_mlp)[0], f(g_post_mlp)[0], f(g_pe)[0]])
    shared = {
        "w_in": f(w_in)[0], "rel": f(rel_table)[0], "w_ao": f(w_attn_out)[0], "cw": f(conv_w)[0],
        "cb": f(conv_b), "w_co": f(w_conv_out)[0], "w_o": f(w_o)[0], "gains": gains,
        "w_up": f(w_up)[0], "w_dn": f(w_down)[0], "w_pe": f(w_pe)[0], "w_pg": f(w_pe_gate)[0],
    }
    in_maps = []
    for c in range(8):
        m = dict(shared)
        m["xp"] = x_prompt[2 * c:2 * c + 2]
        m["xs"] = x_sample[4 * c:4 * c + 4].reshape(128, D)
        m["ck"] = cache_k[0, 4 * c:4 * c + 4].reshape(4, 512, ATT)
        m["cv"] = cache_v[0, 4 * c:4 * c + 4].reshape(4, 512, ATT)
        m["sc"] = state_conv[0, 4 * c:4 * c + 4].reshape(8, CONV)
        m["pp"] = p_prompt[0, 2 * c:2 * c + 2]
        m["ps"] = p_sample[0, 4 * c:4 * c + 4].reshape(128, DPLE)
        in_maps.append(m)
    nc = _get_nc()
    res = run_bass_kernel_spmd(nc, in_maps, core_ids=list(range(8)))
    r = res.results
    cat = lambda k: np.concatenate([np.asarray(r[c][k]) for c in range(8)], axis=0)
    y_prompt = cat("yp")
    y_sample = cat("ys").reshape(32, 32, D)
    k_prompt = cat("kp").reshape(1, 16, 512, NH, HD)
    v_prompt = cat("vp").reshape(1, 16, 512, NH, HD)
    conv_prompt = cat("cp").reshape(1, 16, 2, CONV)
    k_sample = cat("ks").reshape(1, 32, 32, NH, HD)
    v_sample = cat("vs").reshape(1, 32, 32, NH, HD)
    conv_sample = cat("cs").reshape(1, 32, 2, CONV)
    return (y_prompt, y_sample, k_prompt, v_prompt, conv_prompt, k_sample, v_sample, conv_sample)
```
